# Optimizing a Trainium2 kernel written in Bass

```python
import math
import jax, jax.numpy as jnp
from jax import lax
import numpy as np

D_MODEL = 1024
BATCH = 32
SEQ = 2048
DEPTH = 1

CHUNK = 64
GMLP_BLOCK = 128
GMLP_GROUPS = 4
GMLP_WIDTH = D_MODEL
GMLP_GROUP_DIM = GMLP_WIDTH // GMLP_GROUPS
SB_HEADS = 16
SB_HEAD_DIM = 64
SB_WIDTH = SB_HEADS * SB_HEAD_DIM
SB_QBLOCK = 128
D_FF = 2816
N_BRANCHES = 2
D_IN = 2 * GMLP_WIDTH + 3 * SB_WIDTH + N_BRANCHES * D_MODEL
EPS = 1e-6

kernel_name = "hybrid_gmlp_stickbreaking_macaron_block"


def rmsnorm(x, g):
    xf = x.astype(jnp.float32)
    y = xf * lax.rsqrt(jnp.mean(xf * xf, axis=-1, keepdims=True) + EPS)
    return (y * g.astype(jnp.float32)).astype(x.dtype)


def layernorm(x, g, b):
    xf = x.astype(jnp.float32)
    mu = jnp.mean(xf, axis=-1, keepdims=True)
    var = jnp.mean(jnp.square(xf - mu), axis=-1, keepdims=True)
    y = (xf - mu) * lax.rsqrt(var + EPS)
    return (y * g.astype(jnp.float32) + b.astype(jnp.float32)).astype(x.dtype)


def swiglu(x, w_gate, w_up, w_down):
    return (jax.nn.silu(x @ w_gate) * (x @ w_up)) @ w_down


def gmlp_spatial_gating(a, ln_g, ln_b, w_s, b_s):
    bsz, seq, _ = a.shape
    nblk = seq // GMLP_BLOCK
    u, v = jnp.split(a, 2, axis=-1)
    u = u.reshape(bsz, nblk, GMLP_BLOCK, GMLP_GROUPS, GMLP_GROUP_DIM)
    v = v.reshape(bsz, nblk, GMLP_BLOCK, GMLP_GROUPS, GMLP_GROUP_DIM)
    v = layernorm(v, ln_g, ln_b)
    pos = jnp.arange(GMLP_BLOCK)
    mask = (pos[None, :] // CHUNK) <= (pos[:, None] // CHUNK)
    w_masked = jnp.where(mask[None], w_s, jnp.zeros((), w_s.dtype))
    s_mix = jnp.einsum('gts,bnsgc->bntgc', w_masked, v) + jnp.transpose(b_s)[None, None, :, :, None]
    return (u * s_mix).reshape(bsz, seq, GMLP_WIDTH)


def stick_breaking_attention(q, k, v):
    q = jnp.transpose(q, (0, 2, 1, 3))
    k = jnp.transpose(k, (0, 2, 1, 3))
    v = jnp.transpose(v, (0, 2, 1, 3))
    seq = q.shape[2]
    scale = 1.0 / math.sqrt(SB_HEAD_DIM)
    outs = []
    for blk in range(seq // SB_QBLOCK):
        q0 = blk * SB_QBLOCK
        kend = q0 + SB_QBLOCK
        qb = q[:, :, q0:kend]
        kb = k[:, :, :kend]
        vb = v[:, :, :kend]
        z = jnp.einsum('bhtd,bhsd->bhts', qb, kb).astype(jnp.float32) * scale
        t_idx = q0 + jnp.arange(SB_QBLOCK)[:, None]
        s_idx = jnp.arange(kend)[None, :]
        causal = s_idx < t_idx
        log_keep = jnp.where(causal, -jax.nn.softplus(z), 0.0)
        log_w = jax.nn.log_sigmoid(z) + lax.cumsum(log_keep, axis=3, reverse=True) - log_keep
        w = jnp.where(causal, jnp.exp(log_w), 0.0)
        outs.append(jnp.einsum('bhts,bhsd->bhtd', w.astype(vb.dtype), vb))
    o = jnp.concatenate(outs, axis=2)
    return jnp.transpose(o, (0, 2, 1, 3))


def setup_inputs(seed: int = 0) -> dict:
    key = jax.random.key(seed)
    ks = jax.random.split(key, 24)
    f32 = jnp.float32

    def nrm(k, shape, fan_in):
        return jax.random.normal(k, shape, f32) * (fan_in ** -0.5)

    def gain(k, shape):
        return 1.0 + 0.05 * jax.random.normal(k, shape, f32)

    L = DEPTH
    return {
        "x": jax.random.normal(ks[0], (BATCH, SEQ, D_MODEL), f32),
        "ff1_norm": gain(ks[1], (L, D_MODEL)),
        "ff1_w_gate": nrm(ks[2], (L, D_MODEL, D_FF), D_MODEL),
        "ff1_w_up": nrm(ks[3], (L, D_MODEL, D_FF), D_MODEL),
        "ff1_w_down": nrm(ks[4], (L, D_FF, D_MODEL), D_FF),
        "mix_norm": gain(ks[5], (L, D_MODEL)),
        "w_in": nrm(ks[6], (L, D_MODEL, D_IN), D_MODEL),
        "b_gate": 0.05 * jax.random.normal(ks[7], (L, N_BRANCHES * D_MODEL), f32),
        "gmlp_ln_g": gain(ks[8], (L, GMLP_GROUPS, GMLP_GROUP_DIM)),
        "gmlp_ln_b": 0.05 * jax.random.normal(ks[9], (L, GMLP_GROUPS, GMLP_GROUP_DIM), f32),
        "gmlp_w_s": nrm(ks[10], (L, GMLP_GROUPS, GMLP_BLOCK, GMLP_BLOCK), GMLP_BLOCK),
        "gmlp_b_s": 1.0 + 0.1 * jax.random.normal(ks[11], (L, GMLP_GROUPS, GMLP_BLOCK), f32),
        "w_branch_a": nrm(ks[12], (L, GMLP_WIDTH, D_MODEL), GMLP_WIDTH),
        "w_branch_b": nrm(ks[13], (L, SB_WIDTH, D_MODEL), SB_WIDTH),
        "w_out": nrm(ks[14], (L, D_MODEL, D_MODEL), D_MODEL),
        "ff2_norm": gain(ks[15], (L, D_MODEL)),
        "ff2_w_gate": nrm(ks[16], (L, D_MODEL, D_FF), D_MODEL),
        "ff2_w_up": nrm(ks[17], (L, D_MODEL, D_FF), D_MODEL),
        "ff2_w_down": nrm(ks[18], (L, D_FF, D_MODEL), D_FF),
        "final_norm": gain(ks[19], (D_MODEL,)),
    }


def reference(x, ff1_norm, ff1_w_gate, ff1_w_up, ff1_w_down, mix_norm, w_in, b_gate,
              gmlp_ln_g, gmlp_ln_b, gmlp_w_s, gmlp_b_s, w_branch_a, w_branch_b, w_out,
              ff2_norm, ff2_w_gate, ff2_w_up, ff2_w_down, final_norm):
    bsz, seq, _ = x.shape
    splits = [2 * GMLP_WIDTH, 2 * GMLP_WIDTH + SB_WIDTH, 2 * GMLP_WIDTH + 2 * SB_WIDTH,
              2 * GMLP_WIDTH + 3 * SB_WIDTH]
    for l in range(DEPTH):
        x = x + 0.5 * swiglu(rmsnorm(x, ff1_norm[l]), ff1_w_gate[l], ff1_w_up[l], ff1_w_down[l])

        h = rmsnorm(x, mix_norm[l])
        proj = h @ w_in[l]
        a_in, q, k, v, g_logits = jnp.split(proj, splits, axis=-1)
        y_a = gmlp_spatial_gating(jax.nn.gelu(a_in, approximate=False),
                                  gmlp_ln_g[l], gmlp_ln_b[l], gmlp_w_s[l], gmlp_b_s[l])
        q = q.reshape(bsz, seq, SB_HEADS, SB_HEAD_DIM)
        k = k.reshape(bsz, seq, SB_HEADS, SB_HEAD_DIM)
        v = v.reshape(bsz, seq, SB_HEADS, SB_HEAD_DIM)
        y_b = stick_breaking_attention(q, k, v).reshape(bsz, seq, SB_WIDTH)
        gates = jax.nn.sigmoid(g_logits + b_gate[l])
        g_a, g_b = jnp.split(gates, 2, axis=-1)
        merged = g_a * (y_a @ w_branch_a[l]) + g_b * (y_b @ w_branch_b[l])
        x = x + merged @ w_out[l]

        x = x + 0.5 * swiglu(rmsnorm(x, ff2_norm[l]), ff2_w_gate[l], ff2_w_up[l], ff2_w_down[l])
    return rmsnorm(x, final_norm)
```

```python
import numpy as np
from contextlib import ExitStack
import concourse.bass as bass
import concourse.mybir as mybir
from concourse.bass_utils import run_bass_kernel_spmd

F32 = mybir.dt.float32
BF16 = mybir.dt.bfloat16
AF = mybir.ActivationFunctionType
ALU = mybir.AluOpType
ENGS = ("pe", "act", "dve", "pool", "sp")

D = 1024
DFF = 2816
NF = DFF // 128
EPS = 1e-6
NEG = -29952.0


class Op:
    __slots__ = ("eng", "fn", "deps", "signal", "semval", "dma", "dma_val")

    def __init__(self, eng, fn):
        self.eng = eng
        self.fn = fn
        self.deps = []
        self.signal = False
        self.semval = 0
        self.dma = None
        self.dma_val = 0


class Prog:
    def __init__(self):
        self.ops = {e: [] for e in ENGS}
        self.last_w = {}
        self.readers = {}
        self.dma_cnt = {}

    def op(self, eng, fn, reads=(), writes=(), dma=None):
        o = Op(eng, fn)
        deps = {}
        for r in reads:
            w = self.last_w.get(r)
            if w is not None:
                deps[id(w)] = (w, True)
        for r in writes:
            w = self.last_w.get(r)
            if w is not None and id(w) not in deps:
                deps[id(w)] = (w, False)
            for rd in self.readers.get(r, ()):
                if id(rd) not in deps:
                    deps[id(rd)] = (rd, False)
        for d, raw in deps.values():
            if d is o:
                continue
            if d.dma is not None:
                o.deps.append(d)
            elif d.eng != eng:
                d.signal = True
                o.deps.append(d)
            elif raw and eng != "pe":
                d.signal = True
                o.deps.append(d)
        for r in reads:
            self.readers.setdefault(r, []).append(o)
        for r in writes:
            self.last_w[r] = o
            self.readers[r] = []
        if dma is not None:
            o.dma = dma
            self.dma_cnt[dma] = self.dma_cnt.get(dma, 0) + 16
            o.dma_val = self.dma_cnt[dma]
        self.ops[eng].append(o)
        return o

    def emit(self, nc, stack):
        sems = {}
        for e in ENGS:
            sems[e] = stack.enter_context(nc.semaphore("s_" + e))
        for i, d in enumerate(self.dma_cnt):
            sems[("dma", d)] = stack.enter_context(nc.semaphore("d%d" % i))
        for e in ENGS:
            c = 0
            for o in self.ops[e]:
                if o.signal:
                    c += 1
                    o.semval = c
        block = stack.enter_context(nc.Block())

        def run(eng_name, eng):
            waited = {}
            for o in self.ops[eng_name]:
                need = {}
                for d in o.deps:
                    if d.dma is not None:
                        k, v = ("dma", d.dma), d.dma_val
                    else:
                        k, v = d.eng, d.semval
                    if waited.get(k, 0) >= v:
                        continue
                    if need.get(k, 0) < v:
                        need[k] = v
                for k, v in need.items():
                    eng.wait_ge(sems[k], v)
                    waited[k] = v
                ins = o.fn(eng)
                if ins is None:
                    continue
                if o.signal:
                    ins.then_inc(sems[eng_name], 1)
                if o.dma is not None:
                    ins.then_inc(sems[("dma", o.dma)], 16)

        @block.tensor
        def _(e):
            run("pe", e)

        @block.scalar
        def _(e):
            run("act", e)

        @block.vector
        def _(e):
            run("dve", e)

        @block.gpsimd
        def _(e):
            run("pool", e)

        @block.sync
        def _(e):
            run("sp", e)


class Ring:
    def __init__(self, items):
        self.items = items
        self.i = 0

    def next(self):
        it = self.items[self.i % len(self.items)]
        self.i += 1
        return it


class Buf:
    __slots__ = ("ap", "res")

    def __init__(self, ap, res):
        self.ap = ap
        self.res = res


W_SPECS = [
    ("f1g", 22528, 11264), ("f1u", 22528, 11264), ("f1d", 22528, 11264),
    ("win", 57344, 8192), ("wa", 8192, 8192), ("wb", 8192, 8192), ("wo", 8192, 8192),
    ("f2g", 22528, 11264), ("f2u", 22528, 11264), ("f2d", 22528, 11264),
]


def build_nc(NB, S):
    NT = S // 512
    NBLK = S // 128
    nc = bass.Bass("TRN2", target_bir_lowering=False)
    P = Prog()

    def din(name, shape, dt=F32):
        return nc.dram_tensor(name, shape, dt, kind="ExternalInput").ap()

    xT = din("xT", [NB, 128, 8 * S])
    oT = nc.dram_tensor("oT", [NB, 128, 8 * S], F32, kind="ExternalOutput").ap()
    w32 = {n: din(n, [128, c]) for n, c, _ in W_SPECS}
    wsc = {n: nc.dram_tensor("s_" + n, [128, c], BF16, kind="Internal").ap() for n, c, _ in W_SPECS}
    wchunk = {n: ch for n, _, ch in W_SPECS}
    d_norms = din("norms", [128, 32])
    d_bgate = din("bgate", [128, 16])
    d_lng = din("lng", [128, 1024])
    d_lnb = din("lnb", [128, 1024])
    d_wst = din("wst", [128, 512])
    d_bsb = din("bsb", [128, 512])
    d_cmat = din("cmat", [128, 640])
    d_cmask = din("cmask", [128, 896])
    d_gmask = din("gmask", [128, 512])

    with ExitStack() as st:
        def sb(name, shape, dt):
            return st.enter_context(nc.sbuf_tensor(name, shape, dt))

        X = sb("X", [128, 8, S], F32)
        H = sb("H", [128, 8, S], BF16)
        BIG = sb("BIG", [128, 24576], BF16)
        MX1 = sb("MX1", [128, 4096], BF16)
        MX2 = sb("MX2", [128, 8192], BF16)
        T32 = sb("T32", [128, 5, 512], F32)
        T16 = sb("T16", [128, 6, 512], BF16)
        WR = sb("WR", [128, 4, 1024], BF16)
        CMAT = sb("CMAT", [128, 640], BF16)
        WIDE = sb("WIDE", [128, 896], BF16)
        WMT = sb("WMT", [128, 512], BF16)
        BSB = sb("BSB", [128, 512], F32)
        LNG = sb("LNG", [128, 1024], F32)
        LNB = sb("LNB", [128, 1024], F32)
        NRM = sb("NRM", [128, 32], F32)
        BGT = sb("BGT", [128, 16], F32)
        STT = sb("STT", [128, 2, 4, 6], F32)
        MV = sb("MV", [128, 2, 4, 2], F32)
        RS = sb("RS", [128, 2, 4], F32)
        PP = [st.enter_context(nc.psum_tensor("pp%d" % i, [128, 1024], F32)) for i in range(4)]
        PSB = [PP[i // 2][:, (i % 2) * 512:(i % 2 + 1) * 512] for i in range(8)]
        PP3 = [PP[i][:].rearrange("q (e t) -> q e t", e=2) for i in range(4)]

        psA = Ring([Buf(PSB[i], ("ps", i)) for i in range(4)])
        psB = Ring([Buf(PSB[i], ("ps", i)) for i in (4, 5)])
        psC = Ring([Buf(PSB[i], ("ps", i)) for i in (6, 7)])
        t32 = Ring([Buf(T32[:, i, :], ("t32", i)) for i in range(5)])
        t16 = Ring([Buf(T16[:, i, :], ("t16", i)) for i in range(6)])
        wring = Ring([Buf(WR[:, i, :], ("wr", i)) for i in range(4)])
        wdring = Ring([Buf(MX2[:, i * 2816:(i + 1) * 2816], tuple(("M2", k) for k in range(11 * i, 11 * i + 11)))
                       for i in range(2)])
        onesm = CMAT[:, 0:128]
        ident = CMAT[:, 128:256]
        uneg = CMAT[:, 256:384]
        negones = CMAT[:, 384:512]
        zerom = CMAT[:, 512:640]

        def xr(c, g):
            return ("X", c, g)

        def hr(c, g):
            return ("H", c, g)

        def tsl(g):
            return slice(g * 512, (g + 1) * 512)

        def bres(k):
            return [("B", k, 0), ("B", k, 1)]

        def reslist(r):
            return list(r) if isinstance(r, tuple) and r and isinstance(r[0], tuple) else [r]

        def load_cast(dst_ap, src_ap, ncols, dres):
            done = 0
            while done < ncols:
                n = min(512, ncols - done)
                tb = t32.next()
                P.op("sp", (lambda tb=tb, done=done, n=n: lambda e: e.dma_start(out=tb.ap[:, 0:n], in_=src_ap[:, done:done + n]))(),
                     writes=[tb.res], dma=tb.res)
                P.op("dve", (lambda tb=tb, done=done, n=n: lambda e: e.tensor_copy(out=dst_ap[:, done:done + n], in_=tb.ap[:, 0:n]))(),
                     reads=[tb.res], writes=[dres])
                done += n

        load_cast(CMAT[:], d_cmat, 640, "CMAT")
        load_cast(WIDE[:], d_cmask, 896, "WIDE")
        for dst, src, nm in ((BSB, d_bsb, "BSB"), (LNG, d_lng, "LNG"), (LNB, d_lnb, "LNB"), (NRM, d_norms, "NRM"),
                             (BGT, d_bgate, "BGT")):
            P.op("sp", (lambda dst=dst, src=src: lambda e: e.dma_start(out=dst[:], in_=src))(), writes=[nm], dma=nm)
        ta, tb_ = t32.next(), t32.next()
        P.op("sp", lambda e: e.dma_start(out=ta.ap, in_=d_wst), writes=[ta.res], dma=ta.res)
        P.op("sp", lambda e: e.dma_start(out=tb_.ap, in_=d_gmask), writes=[tb_.res], dma=tb_.res)
        P.op("dve", lambda e: e.tensor_tensor(out=WMT[:], in0=ta.ap, in1=tb_.ap, op=ALU.mult),
             reads=[ta.res, tb_.res], writes=["WMT"])

        wchunks = {n: [] for n, _, _ in W_SPECS}
        conv_order = []

        def add_chunks(n, bounds):
            for c0, c1 in bounds:
                wchunks[n].append((c0, c1))
                conv_order.append((n, len(wchunks[n]) - 1, c0, c1))

        gb = [(0, 2048), (2048, 6144), (6144, 14336), (14336, 22528)]
        for (a0, a1) in gb:
            add_chunks("f1g", [(a0, a1)])
            add_chunks("f1u", [(a0, a1)])
        add_chunks("f1d", [(k * 5632, (k + 1) * 5632) for k in range(4)])
        add_chunks("win", [(k * 8192, (k + 1) * 8192) for k in (4, 2, 3)])
        add_chunks("win", [(k * 8192, (k + 1) * 8192) for k in (0, 1, 5, 6)])
        for n in ("wa", "wb", "wo"):
            add_chunks(n, [(0, 8192)])
        for n in ("f2g", "f2u"):
            add_chunks(n, [(k * 5632, (k + 1) * 5632) for k in range(4)])
        add_chunks("f2d", [(k * 5632, (k + 1) * 5632) for k in range(4)])
        def load_x(b, g):
            P.op("sp", (lambda b=b, g=g: lambda e: e.dma_start(
                out=X[:, :, tsl(g)], in_=xT[b].rearrange("p (c s) -> p c s", s=S)[:, :, tsl(g)]))(),
                writes=[xr(c, g) for c in range(8)] + [("xld", g)], dma=("xin", g))

        first_tiles = list(range(min(2, NT)))
        for g in first_tiles:
            load_x(0, g)
        for ci, (n, k, c0, c1) in enumerate(conv_order):
            extra = [("xld", g) for g in first_tiles] if ci >= 2 else []
            P.op("pool", (lambda n=n, c0=c0, c1=c1: lambda e: e.dma_start(out=wsc[n][:, c0:c1], in_=w32[n][:, c0:c1]))(),
                 reads=extra, writes=[("sc", n, k), ("cvslot", ci % 2)], dma=("cv", ci % 2))

        def load_cols(dst_buf, name, c0, ncols):
            rd = [("sc", name, k) for k, (a0, a1) in enumerate(wchunks[name]) if a0 < c0 + ncols and c0 < a1]
            assert rd
            P.op("sp", lambda e: e.dma_start(out=dst_buf.ap, in_=wsc[name][:, c0:c0 + ncols]),
                 reads=rd, writes=reslist(dst_buf.res), dma=reslist(dst_buf.res)[0])

        def load_panel(name, pidx):
            w = wring.next()
            load_cols(w, name, pidx * 1024, 1024)
            return w

        def mm_group(ps, pairs, reads, start=True, stop=True, out_ap=None):
            out_ap = ps.ap if out_ap is None else out_ap

            def fn(e):
                n = len(pairs)
                ins = None
                for i, (l, r) in enumerate(pairs):
                    ins = e.matmul(out_ap, lhsT=l, rhs=r, start=(start and i == 0), stop=(stop and i == n - 1))
                return ins
            P.op("pe", fn, reads=reads, writes=[ps.res])

        def rmsnorm_tile(g, ncol, final=False):
            ps = psC.next()
            for c in range(8):
                sq = t16.next()
                P.op("act", (lambda c=c, sq=sq: lambda e: e.activation(out=sq.ap, in_=X[:, c, tsl(g)], func=AF.Square))(),
                     reads=[xr(c, g)], writes=[sq.res])
                P.op("pe", (lambda c=c, sq=sq: lambda e: e.matmul(ps.ap, lhsT=onesm, rhs=sq.ap, start=(c == 0), stop=(c == 7)))(),
                     reads=[sq.res, "CMAT"], writes=[ps.res])
            R = t32.next()
            P.op("act", lambda e: e.activation(out=R.ap, in_=ps.ap, func=AF.Ln, bias=EPS), reads=[ps.res], writes=[R.res])
            P.op("act", lambda e: e.activation(out=R.ap, in_=R.ap, func=AF.Exp, scale=-0.5), reads=[R.res], writes=[R.res])
            for c in range(8):
                if final:
                    P.op("dve", (lambda c=c: lambda e: e.scalar_tensor_tensor(
                        out=X[:, c, tsl(g)], in0=X[:, c, tsl(g)], scalar=NRM[:, ncol * 8 + c:ncol * 8 + c + 1], in1=R.ap,
                        op0=ALU.mult, op1=ALU.mult))(), reads=[xr(c, g), R.res, "NRM"], writes=[xr(c, g)])
                else:
                    P.op("dve", (lambda c=c: lambda e: e.scalar_tensor_tensor(
                        out=H[:, c, tsl(g)], in0=X[:, c, tsl(g)], scalar=NRM[:, ncol * 8 + c:ncol * 8 + c + 1], in1=R.ap,
                        op0=ALU.mult, op1=ALU.mult))(), reads=[xr(c, g), R.res, "NRM"], writes=[hr(c, g)])

        def ffn_group(tiles, ng, nu, nd, ncol, hook=None):
            for g in tiles:
                rmsnorm_tile(g, ncol)
            for f in range(NF):
                if hook is not None and f == 6:
                    hook()
                wg = load_panel(ng, f)
                wu = load_panel(nu, f)
                for ti, g in enumerate(tiles):
                    pg, pu = psA.next(), psA.next()
                    hreads = [hr(kc, g) for kc in range(8)]
                    mm_group(pg, [(wg.ap[:, kc * 128:(kc + 1) * 128], H[:, kc, tsl(g)]) for kc in range(8)], [wg.res] + hreads)
                    mm_group(pu, [(wu.ap[:, kc * 128:(kc + 1) * 128], H[:, kc, tsl(g)]) for kc in range(8)], [wu.res] + hreads)
                    sg = t32.next()
                    P.op("act", (lambda pg=pg, sg=sg: lambda e: e.activation(out=sg.ap, in_=pg.ap, func=AF.Silu))(),
                         reads=[pg.res], writes=[sg.res])
                    col = f * 1024 + ti * 512
                    P.op("dve", (lambda pu=pu, sg=sg, col=col: lambda e: e.tensor_tensor(
                        out=BIG[:, col:col + 512], in0=sg.ap, in1=pu.ap, op=ALU.mult))(),
                        reads=[sg.res, pu.res], writes=bres(col // 512))
            for dc in range(8):
                wd = wdring.next()
                load_cols(wd, nd, dc * 2816, 2816)
                for ti, g in enumerate(tiles):
                    pd = psB.next()
                    mm_group(pd, [(wd.ap[:, f * 128:(f + 1) * 128], BIG[:, f * 1024 + ti * 512:f * 1024 + ti * 512 + 512])
                                  for f in range(NF)],
                             list(wd.res) + [r for f in range(NF) for r in bres(2 * f + ti)])
                    P.op("dve", (lambda pd=pd, dc=dc, g=g: lambda e: e.scalar_tensor_tensor(
                        out=X[:, dc, tsl(g)], in0=pd.ap, scalar=0.5, in1=X[:, dc, tsl(g)], op0=ALU.mult, op1=ALU.add))(),
                        reads=[pd.res, xr(dc, g)], writes=[xr(dc, g)])

        def ffn(ng, nu, nd, ncol, hook=None):
            for g0 in range(0, NT, 2):
                ffn_group(list(range(g0, min(g0 + 2, NT))), ng, nu, nd, ncol, hook if g0 == 0 else None)

        QT = MX1[:, 0:2048]
        KT = MX1[:, 2048:4096]

        def attention():
            zring = Ring([0, 1, 2])
            sparing = Ring(list(range(8)))
            ering = Ring([0, 1])
            bankring = Ring([Buf(PSB[i], ("ps", i)) for i in range(6)])

            def m2res(i, e_=None):
                if e_ is None:
                    return [("M2", 4 * i + k) for k in range(4)]
                return [("M2", 4 * i + 2 * e_), ("M2", 4 * i + 2 * e_ + 1)]

            def spa_ap(i):
                return MX2[:, i * 1024:(i + 1) * 1024].rearrange("q (e t) -> q e t", e=2)

            def proj_qk(hp, banks):
                wq = load_panel("win", 16 + hp)
                for g in range(NT):
                    ps = banks.next()
                    mm_group(ps, [(wq.ap[:, kc * 128:(kc + 1) * 128], H[:, kc, tsl(g)]) for kc in range(8)],
                             [wq.res] + [hr(kc, g) for kc in range(8)])
                    P.op("dve", (lambda ps=ps, g=g: lambda e: e.tensor_scalar(out=QT[:, tsl(g)], in0=ps.ap, scalar1=0.125, scalar2=None,
                                                                              op0=ALU.mult))(),
                         reads=[ps.res], writes=[("M1", g)])
                wk = load_panel("win", 24 + hp)
                for g in range(NT):
                    ps = banks.next()
                    mm_group(ps, [(wk.ap[:, kc * 128:(kc + 1) * 128], H[:, kc, tsl(g)]) for kc in range(8)],
                             [wk.res] + [hr(kc, g) for kc in range(8)])
                    P.op("dve", (lambda ps=ps, g=g: lambda e: e.tensor_copy(out=KT[:, tsl(g)], in_=ps.ap))(),
                         reads=[ps.res], writes=[("M1", 4 + g)])

            for hp in range(8):
                hpl = hp % 4
                if hpl == 0:
                    wring.i += (-wring.i) % 4
                    wv = [load_panel("win", 32 + hp + i) for i in range(4)]
                    assert [w.res for w in wv] == [("wr", i) for i in range(4)]
                    for blk in range(NBLK):
                        ps = bankring.next()
                        g = blk // 4
                        mm_group(ps, [(H[:, kc, blk * 128:(blk + 1) * 128], WR[:, 0:4, kc * 128:(kc + 1) * 128]) for kc in range(8)],
                                 [w.res for w in wv] + [hr(kc, g) for kc in range(8)])
                        col = 16384 + blk * 512
                        P.op("dve", (lambda ps=ps, col=col: lambda e: e.tensor_copy(out=BIG[:, col:col + 512], in_=ps.ap))(),
                             reads=[ps.res], writes=bres(col // 512))
                if hp == 0:
                    proj_qk(0, bankring)
                units = []
                for qt in range(NT):
                    nkb = 4 * (qt + 1)
                    for idx, j in enumerate(range(nkb - 1, -1, -1)):
                        c0 = 128 * (j - 4 * qt) if j >= 4 * qt else 0
                        units.append(dict(qt=qt, idx=idx, j=j, c0=c0, last=(idx == nkb - 1), gi=len(units)))
                yres = [("ps", 6), ("ps", 7)]

                def s1a(u):
                    zp = zring.next()
                    u["zp"] = zp
                    c0, j, qt = u["c0"], u["j"], u["qt"]
                    for e_ in range(2):
                        rows = slice(64 * e_, 64 * e_ + 64)
                        pairs = [(KT[rows, j * 128:(j + 1) * 128], QT[rows, qt * 512 + c0:(qt + 1) * 512])]
                        rds = [("M1", 4 + j // 4), ("M1", qt)]
                        if j >= 4 * qt:
                            pairs.append((ident, WIDE[:, 384:896 - c0]))
                            rds += ["CMAT", "WIDE"]
                        zb = Buf(PSB[2 * zp + e_][:, c0:512], ("ps", 2 * zp + e_))
                        mm_group(zb, pairs, rds, start=True, stop=False)
                    ei = ering.next()
                    u["ei"] = ei
                    P.op("act", lambda e: e.activation(out=T32[:, 2 * ei:2 * ei + 2, c0:512], in_=PP3[zp][:, :, c0:512], func=AF.Exp),
                         reads=[("ps", 2 * zp), ("ps", 2 * zp + 1)], writes=[("t32", 2 * ei), ("t32", 2 * ei + 1)])

                def s1b(u):
                    si = sparing.next()
                    u["sp"] = si
                    c0, ei = u["c0"], u["ei"]
                    P.op("act", lambda e: e.activation(out=spa_ap(si)[:, :, c0:512], in_=T32[:, 2 * ei:2 * ei + 2, c0:512], func=AF.Ln, bias=1.0),
                         reads=[("t32", 2 * ei), ("t32", 2 * ei + 1)], writes=m2res(si))

                def s2pe(u):
                    c0, zp, si, idx = u["c0"], u["zp"], u["sp"], u["idx"]
                    sv = u["gi"] % 2
                    nv = 1 - sv
                    for e_ in range(2):
                        pairs = [(uneg, spa_ap(si)[:, e_, c0:512])]
                        rds = m2res(si, e_) + ["CMAT"]
                        if idx > 0:
                            pairs.append((negones, T16[:, 2 * sv + e_, c0:512]))
                            rds.append(("t16", 2 * sv + e_))
                        zb = Buf(PSB[2 * zp + e_][:, c0:512], ("ps", 2 * zp + e_))
                        mm_group(zb, pairs, rds, start=False, stop=True)
                    if not u["last"]:
                        nres = [("t16", 2 * nv), ("t16", 2 * nv + 1)]
                        if c0 > 0:
                            P.op("dve", lambda e: e.memset(T16[:, 2 * nv:2 * nv + 2, 0:c0], 0.0), writes=nres)
                        if idx == 0:
                            P.op("dve", lambda e: e.tensor_copy(out=T16[:, 2 * nv:2 * nv + 2, c0:512], in_=spa_ap(si)[:, :, c0:512]),
                                 reads=m2res(si), writes=nres)
                        else:
                            P.op("dve", lambda e: e.tensor_tensor(out=T16[:, 2 * nv:2 * nv + 2, c0:512], in0=T16[:, 2 * sv:2 * sv + 2, c0:512],
                                                                  in1=spa_ap(si)[:, :, c0:512], op=ALU.add),
                                 reads=[("t16", 2 * sv), ("t16", 2 * sv + 1)] + m2res(si), writes=nres)

                def s2act(u):
                    ai = sparing.next()
                    u["a"] = ai
                    c0, zp = u["c0"], u["zp"]
                    P.op("act", lambda e: e.activation(out=spa_ap(ai)[:, :, c0:512], in_=PP3[zp][:, :, c0:512], func=AF.Exp),
                         reads=[("ps", 2 * zp), ("ps", 2 * zp + 1)], writes=m2res(ai))

                def s3(u):
                    c0, ai, j, idx, qt = u["c0"], u["a"], u["j"], u["idx"], u["qt"]
                    if idx == 0:
                        for e_ in range(2):
                            P.op("pe", (lambda e_=e_, qt=qt: lambda e: e.matmul(PSB[6 + e_], lhsT=zerom, rhs=QT[:, tsl(qt)], start=True, stop=False))(),
                                 reads=[("M1", qt), "CMAT"], writes=[yres[e_]])
                    vcol = 16384 + j * 512 + hpl * 128
                    for e_ in range(2):
                        yb = Buf(PSB[6 + e_][:, c0:512], yres[e_])
                        mm_group(yb, [(BIG[:, vcol:vcol + 128], spa_ap(ai)[:, e_, c0:512])], m2res(ai, e_) + bres(32 + j),
                                 start=False, stop=u["last"])
                    if u["last"]:
                        for e_ in range(2):
                            rows = slice(64 * e_, 64 * e_ + 64)
                            col = hp * S + qt * 512
                            P.op("dve", (lambda e_=e_, rows=rows, col=col: lambda e: e.tensor_copy(
                                out=BIG[rows, col:col + 512], in_=PSB[6 + e_][rows, :]))(),
                                reads=[yres[e_]], writes=[("B", col // 512, e_)])

                n = len(units)
                for i in range(n + 3):
                    if i == n and hp < 7:
                        fz = units[n - 3]["zp"]
                        proj_qk(hp + 1, Ring([Buf(PSB[2 * fz + k], ("ps", 2 * fz + k)) for k in range(2)]))
                    if i < n:
                        s1a(units[i])
                    if 1 <= i <= n:
                        s2pe(units[i - 1])
                    if 2 <= i <= n + 1:
                        s2act(units[i - 2])
                    if i < n:
                        s1b(units[i])
                    if i >= 3:
                        s3(units[i - 3])

        def yb_res(c, g):
            col = c * S + g * 512
            return [("B", col // 512, 0), ("B", col // 512, 1)]

        def mix_tile(g):
            hreads = [hr(kc, g) for kc in range(8)]
            vw = Buf(BIG[:, 16384:24576], tuple(r for k in range(32, 48) for r in bres(k)))
            load_cols(vw, "win", 8 * 1024, 8192)
            VW = BIG[:, 16384:24576].rearrange("p (n k) -> p n k", k=1024)
            for tb in range(4):
                blk = 4 * g + tb
                par = tb % 2
                vfs = []
                for hf in range(2):
                    ps = psC.next()
                    mm_group(ps, [(H[:, kc, blk * 128:(blk + 1) * 128], VW[:, 4 * hf:4 * hf + 4, kc * 128:(kc + 1) * 128])
                                  for kc in range(8)], list(vw.res) + hreads)
                    VF = t32.next()
                    vfs.append(VF)
                    P.op("act", (lambda ps=ps, VF=VF: lambda e: e.activation(out=VF.ap, in_=ps.ap, func=AF.Gelu))(),
                         reads=[ps.res], writes=[VF.res])
                    for gg in range(2):
                        grp = 2 * hf + gg
                        P.op("dve", (lambda VF=VF, gg=gg, grp=grp, par=par: lambda e: e.bn_stats(
                            out=STT[:, par, grp, :], in_=VF.ap[:, gg * 256:(gg + 1) * 256]))(),
                            reads=[VF.res], writes=[("STT", par, grp)])
                        P.op("dve", (lambda grp=grp, par=par: lambda e: e.bn_aggr(out=MV[:, par, grp, :], in_=STT[:, par, grp, :]))(),
                             reads=[("STT", par, grp)], writes=[("MV", par, grp)])
                mvr = [("MV", par, grp) for grp in range(4)]
                P.op("act", (lambda par=par: lambda e: e.activation(out=RS[:, par, :], in_=MV[:, par, :, 1], func=AF.Ln, bias=EPS))(),
                     reads=mvr, writes=[("RS", par)])
                P.op("act", (lambda par=par: lambda e: e.activation(out=RS[:, par, :], in_=RS[:, par, :], func=AF.Exp, scale=-0.5))(),
                     reads=[("RS", par)], writes=[("RS", par)])
                for hf in range(2):
                    VF = vfs[hf]
                    for gg in range(2):
                        grp = 2 * hf + gg
                        P.op("dve", (lambda VF=VF, gg=gg, grp=grp, par=par: lambda e: e.tensor_scalar(
                            out=VF.ap[:, gg * 256:(gg + 1) * 256], in0=VF.ap[:, gg * 256:(gg + 1) * 256],
                            scalar1=MV[:, par, grp, 0:1], scalar2=RS[:, par, grp:grp + 1], op0=ALU.subtract, op1=ALU.mult))(),
                            reads=[VF.res, ("MV", par, grp), ("RS", par)], writes=[VF.res])
                    P.op("dve", (lambda VF=VF, hf=hf: lambda e: e.tensor_tensor(
                        out=VF.ap, in0=VF.ap, in1=LNG[:, hf * 512:(hf + 1) * 512], op=ALU.mult))(),
                        reads=[VF.res, "LNG"], writes=[VF.res])
                    vcol = 4096 + tb * 1024 + hf * 512
                    P.op("pool", (lambda VF=VF, hf=hf, vcol=vcol: lambda e: e.tensor_tensor(
                        out=MX2[:, vcol:vcol + 512], in0=VF.ap, in1=LNB[:, hf * 512:(hf + 1) * 512], op=ALU.add))(),
                        reads=[VF.res, "LNB"], writes=[("M2", vcol // 256), ("M2", vcol // 256 + 1)])
            for c in range(8):
                w = load_panel("win", c)
                ps = psC.next()
                mm_group(ps, [(w.ap[:, kc * 128:(kc + 1) * 128], H[:, kc, tsl(g)]) for kc in range(8)], [w.res] + hreads)
                P.op("act", (lambda ps=ps, c=c: lambda e: e.activation(out=MX1[:, c * 512:(c + 1) * 512], in_=ps.ap, func=AF.Gelu))(),
                     reads=[ps.res], writes=[("M1", c)])
            for c in range(8):
                grp = c // 2
                ps = psC.next()
                for tb in range(4):
                    vcol = 4096 + tb * 1024 + c * 128
                    P.op("pe", (lambda ps=ps, tb=tb, vcol=vcol, grp=grp: lambda e: e.matmul(
                        ps.ap[:, tb * 128:(tb + 1) * 128], lhsT=MX2[:, vcol:vcol + 128], rhs=WMT[:, grp * 128:(grp + 1) * 128],
                        start=True, stop=True))(),
                        reads=[("M2", vcol // 256), "WMT"], writes=[ps.res])
                T1 = t32.next()
                P.op("dve", (lambda ps=ps, grp=grp, T1=T1: lambda e: e.tensor_tensor(
                    out=T1.ap.rearrange("q (b t) -> q b t", b=4), in0=ps.ap.rearrange("q (b t) -> q b t", b=4),
                    in1=BSB[:, grp * 128:(grp + 1) * 128].unsqueeze(1).broadcast_to([128, 4, 128]), op=ALU.add))(),
                    reads=[ps.res, "BSB"], writes=[T1.res])
                P.op("dve", (lambda c=c, T1=T1: lambda e: e.tensor_tensor(
                    out=MX1[:, c * 512:(c + 1) * 512], in0=T1.ap, in1=MX1[:, c * 512:(c + 1) * 512], op=ALU.mult))(),
                    reads=[T1.res, ("M1", c)], writes=[("M1", c)])
            yar = [("M1", k) for k in range(8)]
            for c in range(8):
                wA = load_panel("wa", c)
                pA = psA.next()
                mm_group(pA, [(wA.ap[:, k * 128:(k + 1) * 128], MX1[:, k * 512:(k + 1) * 512]) for k in range(8)], [wA.res] + yar)
                wga = load_panel("win", 40 + c)
                pga = psA.next()
                mm_group(pga, [(wga.ap[:, kc * 128:(kc + 1) * 128], H[:, kc, tsl(g)]) for kc in range(8)], [wga.res] + hreads)
                GA = t32.next()
                P.op("act", (lambda pga=pga, GA=GA, c=c: lambda e: e.activation(out=GA.ap, in_=pga.ap, func=AF.Sigmoid, bias=BGT[:, c:c + 1]))(),
                     reads=[pga.res, "BGT"], writes=[GA.res])
                P.op("dve", (lambda pA=pA, GA=GA: lambda e: e.tensor_tensor(out=GA.ap, in0=GA.ap, in1=pA.ap, op=ALU.mult))(),
                     reads=[GA.res, pA.res], writes=[GA.res])
                wB = load_panel("wb", c)
                pB = psA.next()
                ybr = []
                for k in range(8):
                    ybr += yb_res(k, g)
                mm_group(pB, [(wB.ap[:, k * 128:(k + 1) * 128], BIG[:, k * S + g * 512:k * S + g * 512 + 512]) for k in range(8)],
                         [wB.res] + ybr)
                wgb = load_panel("win", 48 + c)
                pgb = psA.next()
                mm_group(pgb, [(wgb.ap[:, kc * 128:(kc + 1) * 128], H[:, kc, tsl(g)]) for kc in range(8)], [wgb.res] + hreads)
                GB = t32.next()
                P.op("act", (lambda pgb=pgb, GB=GB, c=c: lambda e: e.activation(out=GB.ap, in_=pgb.ap, func=AF.Sigmoid, bias=BGT[:, 8 + c:9 + c]))(),
                     reads=[pgb.res, "BGT"], writes=[GB.res])
                P.op("dve", (lambda pB=pB, GB=GB: lambda e: e.tensor_tensor(out=GB.ap, in0=GB.ap, in1=pB.ap, op=ALU.mult))(),
                     reads=[GB.res, pB.res], writes=[GB.res])
                P.op("dve", (lambda GA=GA, GB=GB, c=c: lambda e: e.tensor_tensor(
                    out=MX2[:, c * 512:(c + 1) * 512], in0=GA.ap, in1=GB.ap, op=ALU.add))(),
                    reads=[GA.res, GB.res], writes=[("M2", 2 * c), ("M2", 2 * c + 1)])
            mtr = [("M2", k) for k in range(16)]
            for c in range(8):
                wo = load_panel("wo", c)
                ps = psC.next()
                mm_group(ps, [(wo.ap[:, k * 128:(k + 1) * 128], MX2[:, k * 512:(k + 1) * 512]) for k in range(8)], [wo.res] + mtr)
                P.op("dve", (lambda ps=ps, c=c: lambda e: e.tensor_tensor(out=X[:, c, tsl(g)], in0=ps.ap, in1=X[:, c, tsl(g)], op=ALU.add))(),
                     reads=[ps.res, xr(c, g)], writes=[xr(c, g)])

        for b in range(NB):
            late = []
            for g in range(NT):
                if b == 0 and g in first_tiles:
                    continue
                if b == 0:
                    late.append(g)
                else:
                    load_x(b, g)
            ffn("f1g", "f1u", "f1d", 0, hook=(lambda b=b, late=late: [load_x(b, g) for g in late]))
            for g in range(NT):
                rmsnorm_tile(g, 1)
            attention()
            for g in range(NT):
                mix_tile(g)
            ffn("f2g", "f2u", "f2d", 2)
            for g in range(NT):
                rmsnorm_tile(g, 3, final=True)
                P.op("sp", (lambda b=b, g=g: lambda e: e.dma_start(
                    out=oT[b].rearrange("p (c s) -> p c s", s=S)[:, :, tsl(g)], in_=X[:, :, tsl(g)]))(),
                    reads=[xr(c, g) for c in range(8)], writes=[("out", b, g)], dma=("xout", g))
        P.op("sp", lambda e: None, reads=[("out", b, g) for b in range(NB) for g in range(NT)])
        P.emit(nc, st)
    return nc


def panelize(W):
    K, N = W.shape
    KC, NP = K // 128, N // 128
    return np.ascontiguousarray(W.reshape(KC, 128, NP, 128).transpose(1, 2, 0, 3)).reshape(128, NP * KC * 128)


def vec8(v):
    return np.ascontiguousarray(v.reshape(-1, 128).T)


def const_inputs():
    f = np.float32
    s = np.arange(128)
    cmat = np.zeros((128, 640), f)
    cmat[:, 0:128] = 1.0 / 1024.0
    cmat[:, 128:256] = np.eye(128, dtype=f)
    cmat[:, 256:384] = -(s[:, None] >= s[None, :]).astype(f)
    cmat[:, 384:512] = -1.0
    u = np.arange(896) - 384
    cmask = np.where(s[:, None] >= u[None, :], NEG, 0.0).astype(f)
    gm = np.ones((128, 128), f)
    gm[64:, :64] = 0.0
    gmask = np.tile(gm, (1, 4))
    return {"cmat": cmat, "cmask": cmask, "gmask": gmask}


def shared_inputs(inp):
    f = np.float32
    d = {}
    d["f1g"] = panelize(inp["ff1_w_gate"][0])
    d["f1u"] = panelize(inp["ff1_w_up"][0])
    d["f1d"] = panelize(inp["ff1_w_down"][0])
    d["f2g"] = panelize(inp["ff2_w_gate"][0])
    d["f2u"] = panelize(inp["ff2_w_up"][0])
    d["f2d"] = panelize(inp["ff2_w_down"][0])
    d["win"] = panelize(inp["w_in"][0])
    d["wa"] = panelize(inp["w_branch_a"][0])
    d["wb"] = panelize(inp["w_branch_b"][0])
    d["wo"] = panelize(inp["w_out"][0])
    d["norms"] = np.concatenate([vec8(inp["ff1_norm"][0]), vec8(inp["mix_norm"][0]), vec8(inp["ff2_norm"][0]),
                                 vec8(inp["final_norm"])], axis=1).astype(f)
    d["bgate"] = vec8(inp["b_gate"][0]).astype(f)
    d["lng"] = np.ascontiguousarray(np.broadcast_to(inp["gmlp_ln_g"][0].reshape(1, 1024), (128, 1024))).astype(f)
    d["lnb"] = np.ascontiguousarray(np.broadcast_to(inp["gmlp_ln_b"][0].reshape(1, 1024), (128, 1024))).astype(f)
    d["wst"] = np.ascontiguousarray(inp["gmlp_w_s"][0].transpose(2, 0, 1)).reshape(128, 512).astype(f)
    d["bsb"] = np.ascontiguousarray(np.broadcast_to(inp["gmlp_b_s"][0].reshape(1, 512), (128, 512))).astype(f)
    d.update(const_inputs())
    return d


_NC_CACHE = {}


def run(inp, n_cores, NB, S):
    inp = {k: np.asarray(v, dtype=np.float32) for k, v in inp.items()}
    x = inp["x"]
    assert x.shape == (n_cores * NB, S, D)
    shared = shared_inputs(inp)
    key = (NB, S)
    if key not in _NC_CACHE:
        _NC_CACHE[key] = build_nc(NB, S)
    nc = _NC_CACHE[key]
    in_maps = []
    for c in range(n_cores):
        xs = x[c * NB:(c + 1) * NB].reshape(NB, S, 8, 128).transpose(0, 3, 2, 1)
        m = dict(shared)
        m["xT"] = np.ascontiguousarray(xs).reshape(NB, 128, 8 * S)
        in_maps.append(m)
    res = run_bass_kernel_spmd(nc, in_maps, core_ids=list(range(n_cores)))
    outs = []
    for c in range(n_cores):
        o = np.asarray(res.results[c]["oT"]).reshape(NB, 128, 8, S).transpose(0, 3, 2, 1).reshape(NB, S, D)
        outs.append(o)
    return np.ascontiguousarray(np.concatenate(outs, axis=0)).astype(np.float32)


def kernel(**inputs):
    return run(inputs, 8, 4, 2048)
```

```python
import numpy as np
from contextlib import ExitStack
import concourse.bass as bass
import concourse.mybir as mybir
from concourse.bass_utils import run_bass_kernel_spmd

F32 = mybir.dt.float32
BF16 = mybir.dt.bfloat16
AF = mybir.ActivationFunctionType
ALU = mybir.AluOpType
ENGS = ("pe", "act", "dve", "pool", "sp")

D = 1024
DFF = 2816
NF = DFF // 128
EPS = 1e-6
NEG = -29952.0


class Op:
    __slots__ = ("eng", "fn", "deps", "signal", "semval", "dma", "dma_val")

    def __init__(self, eng, fn):
        self.eng = eng
        self.fn = fn
        self.deps = []
        self.signal = False
        self.semval = 0
        self.dma = None
        self.dma_val = 0


class Prog:
    def __init__(self):
        self.ops = {e: [] for e in ENGS}
        self.last_w = {}
        self.readers = {}
        self.dma_cnt = {}

    def op(self, eng, fn, reads=(), writes=(), dma=None):
        o = Op(eng, fn)
        deps = {}
        for r in reads:
            w = self.last_w.get(r)
            if w is not None:
                deps[id(w)] = (w, True)
        for r in writes:
            w = self.last_w.get(r)
            if w is not None and id(w) not in deps:
                deps[id(w)] = (w, False)
            for rd in self.readers.get(r, ()):
                if id(rd) not in deps:
                    deps[id(rd)] = (rd, False)
        for d, raw in deps.values():
            if d is o:
                continue
            if d.dma is not None:
                o.deps.append(d)
            elif d.eng != eng:
                d.signal = True
                o.deps.append(d)
            elif raw and eng != "pe":
                d.signal = True
                o.deps.append(d)
        for r in reads:
            self.readers.setdefault(r, []).append(o)
        for r in writes:
            self.last_w[r] = o
            self.readers[r] = []
        if dma is not None:
            o.dma = dma
            self.dma_cnt[dma] = self.dma_cnt.get(dma, 0) + 16
            o.dma_val = self.dma_cnt[dma]
        self.ops[eng].append(o)
        return o

    def emit(self, nc, stack):
        sems = {}
        for e in ENGS:
            sems[e] = stack.enter_context(nc.semaphore("s_" + e))
        for i, d in enumerate(self.dma_cnt):
            sems[("dma", d)] = stack.enter_context(nc.semaphore("d%d" % i))
        for e in ENGS:
            c = 0
            for o in self.ops[e]:
                if o.signal:
                    c += 1
                    o.semval = c
        block = stack.enter_context(nc.Block())

        def run(eng_name, eng):
            waited = {}
            for o in self.ops[eng_name]:
                need = {}
                for d in o.deps:
                    if d.dma is not None:
                        k, v = ("dma", d.dma), d.dma_val
                    else:
                        k, v = d.eng, d.semval
                    if waited.get(k, 0) >= v:
                        continue
                    if need.get(k, 0) < v:
                        need[k] = v
                for k, v in need.items():
                    eng.wait_ge(sems[k], v)
                    waited[k] = v
                ins = o.fn(eng)
                if ins is None:
                    continue
                if o.signal:
                    ins.then_inc(sems[eng_name], 1)
                if o.dma is not None:
                    ins.then_inc(sems[("dma", o.dma)], 16)

        @block.tensor
        def _(e):
            run("pe", e)

        @block.scalar
        def _(e):
            run("act", e)

        @block.vector
        def _(e):
            run("dve", e)

        @block.gpsimd
        def _(e):
            run("pool", e)

        @block.sync
        def _(e):
            run("sp", e)


class Ring:
    def __init__(self, items):
        self.items = items
        self.i = 0

    def next(self):
        it = self.items[self.i % len(self.items)]
        self.i += 1
        return it


class Buf:
    __slots__ = ("ap", "res")

    def __init__(self, ap, res):
        self.ap = ap
        self.res = res


W_SPECS = [
    ("f1g", 22528, 11264), ("f1u", 22528, 11264), ("f1d", 22528, 11264),
    ("win", 57344, 8192), ("wa", 8192, 8192), ("wb", 8192, 8192), ("wo", 8192, 8192),
    ("f2g", 22528, 11264), ("f2u", 22528, 11264), ("f2d", 22528, 11264),
]


def build_nc(NB, S):
    NT = S // 512
    NBLK = S // 128
    nc = bass.Bass("TRN2", target_bir_lowering=False)
    P = Prog()

    def din(name, shape, dt=F32):
        return nc.dram_tensor(name, shape, dt, kind="ExternalInput").ap()

    xT = din("xT", [NB, 128, 8 * S])
    oT = nc.dram_tensor("oT", [NB, 128, 8 * S], F32, kind="ExternalOutput").ap()
    w32 = {n: din(n, [128, c]) for n, c, _ in W_SPECS}
    wsc = {n: nc.dram_tensor("s_" + n, [128, c], BF16, kind="Internal").ap() for n, c, _ in W_SPECS}
    wchunk = {n: ch for n, _, ch in W_SPECS}
    d_norms = din("norms", [128, 32])
    d_bgate = din("bgate", [128, 16])
    d_lng = din("lng", [128, 1024])
    d_lnb = din("lnb", [128, 1024])
    d_wst = din("wst", [128, 512])
    d_bsb = din("bsb", [128, 512])
    d_cmat = din("cmat", [128, 640])
    d_cmask = din("cmask", [128, 896])
    d_gmask = din("gmask", [128, 512])

    with ExitStack() as st:
        def sb(name, shape, dt):
            return st.enter_context(nc.sbuf_tensor(name, shape, dt))

        X = sb("X", [128, 8, S], F32)
        H = sb("H", [128, 8, S], BF16)
        BIG = sb("BIG", [128, 24576], BF16)
        MX1 = sb("MX1", [128, 4096], BF16)
        MX2 = sb("MX2", [128, 8192], BF16)
        T32 = sb("T32", [128, 5, 512], F32)
        T16 = sb("T16", [128, 6, 512], BF16)
        WR = sb("WR", [128, 4, 1024], BF16)
        CMAT = sb("CMAT", [128, 640], BF16)
        WIDE = sb("WIDE", [128, 896], BF16)
        WMT = sb("WMT", [128, 512], BF16)
        BSB = sb("BSB", [128, 512], F32)
        LNG = sb("LNG", [128, 1024], F32)
        LNB = sb("LNB", [128, 1024], F32)
        NRM = sb("NRM", [128, 32], F32)
        BGT = sb("BGT", [128, 16], F32)
        STT = sb("STT", [128, 2, 4, 6], F32)
        MV = sb("MV", [128, 2, 4, 2], F32)
        RS = sb("RS", [128, 2, 4], F32)
        PP = [st.enter_context(nc.psum_tensor("pp%d" % i, [128, 1024], F32)) for i in range(4)]
        PSB = [PP[i // 2][:, (i % 2) * 512:(i % 2 + 1) * 512] for i in range(8)]
        PP3 = [PP[i][:].rearrange("q (e t) -> q e t", e=2) for i in range(4)]

        psA = Ring([Buf(PSB[i], ("ps", i)) for i in range(4)])
        psB = Ring([Buf(PSB[i], ("ps", i)) for i in (4, 5)])
        psC = Ring([Buf(PSB[i], ("ps", i)) for i in (6, 7)])
        t32 = Ring([Buf(T32[:, i, :], ("t32", i)) for i in range(5)])
        t16 = Ring([Buf(T16[:, i, :], ("t16", i)) for i in range(6)])
        wring = Ring([Buf(WR[:, i, :], ("wr", i)) for i in range(4)])
        wdring = Ring([Buf(MX2[:, i * 2816:(i + 1) * 2816], tuple(("M2", k) for k in range(11 * i, 11 * i + 11)))
                       for i in range(2)])
        onesm = CMAT[:, 0:128]
        ident = CMAT[:, 128:256]
        uneg = CMAT[:, 256:384]
        negones = CMAT[:, 384:512]
        zerom = CMAT[:, 512:640]

        def xr(c, g):
            return ("X", c, g)

        def hr(c, g):
            return ("H", c, g)

        def tsl(g):
            return slice(g * 512, (g + 1) * 512)

        def bres(k):
            return [("B", k, 0), ("B", k, 1)]

        def reslist(r):
            return list(r) if isinstance(r, tuple) and r and isinstance(r[0], tuple) else [r]

        def load_cast(dst_ap, src_ap, ncols, dres):
            done = 0
            while done < ncols:
                n = min(512, ncols - done)
                tb = t32.next()
                P.op("sp", (lambda tb=tb, done=done, n=n: lambda e: e.dma_start(out=tb.ap[:, 0:n], in_=src_ap[:, done:done + n]))(),
                     writes=[tb.res], dma=tb.res)
                P.op("dve", (lambda tb=tb, done=done, n=n: lambda e: e.tensor_copy(out=dst_ap[:, done:done + n], in_=tb.ap[:, 0:n]))(),
                     reads=[tb.res], writes=[dres])
                done += n

        load_cast(CMAT[:], d_cmat, 640, "CMAT")
        load_cast(WIDE[:], d_cmask, 896, "WIDE")
        for dst, src, nm in ((BSB, d_bsb, "BSB"), (LNG, d_lng, "LNG"), (LNB, d_lnb, "LNB"), (NRM, d_norms, "NRM"),
                             (BGT, d_bgate, "BGT")):
            P.op("sp", (lambda dst=dst, src=src: lambda e: e.dma_start(out=dst[:], in_=src))(), writes=[nm], dma=nm)
        ta, tb_ = t32.next(), t32.next()
        P.op("sp", lambda e: e.dma_start(out=ta.ap, in_=d_wst), writes=[ta.res], dma=ta.res)
        P.op("sp", lambda e: e.dma_start(out=tb_.ap, in_=d_gmask), writes=[tb_.res], dma=tb_.res)
        P.op("dve", lambda e: e.tensor_tensor(out=WMT[:], in0=ta.ap, in1=tb_.ap, op=ALU.mult),
             reads=[ta.res, tb_.res], writes=["WMT"])

        wchunks = {n: [] for n, _, _ in W_SPECS}
        conv_order = []

        def add_chunks(n, bounds):
            for c0, c1 in bounds:
                wchunks[n].append((c0, c1))
                conv_order.append((n, len(wchunks[n]) - 1, c0, c1))

        gb = [(0, 2048), (2048, 6144), (6144, 14336), (14336, 22528)]
        for (a0, a1) in gb:
            add_chunks("f1g", [(a0, a1)])
            add_chunks("f1u", [(a0, a1)])
        add_chunks("f1d", [(k * 5632, (k + 1) * 5632) for k in range(4)])
        add_chunks("win", [(k * 8192, (k + 1) * 8192) for k in (4, 2, 3)])
        add_chunks("win", [(k * 8192, (k + 1) * 8192) for k in (0, 1, 5, 6)])
        for n in ("wa", "wb", "wo"):
            add_chunks(n, [(0, 8192)])
        for n in ("f2g", "f2u"):
            add_chunks(n, [(k * 5632, (k + 1) * 5632) for k in range(4)])
        add_chunks("f2d", [(k * 5632, (k + 1) * 5632) for k in range(4)])
        def load_x(b, g):
            P.op("sp", (lambda b=b, g=g: lambda e: e.dma_start(
                out=X[:, :, tsl(g)], in_=xT[b].rearrange("p (c s) -> p c s", s=S)[:, :, tsl(g)]))(),
                writes=[xr(c, g) for c in range(8)] + [("xld", g)], dma=("xin", g))

        first_tiles = list(range(min(2, NT)))
        for g in first_tiles:
            load_x(0, g)
        for ci, (n, k, c0, c1) in enumerate(conv_order):
            extra = [("xld", g) for g in first_tiles] if ci >= 2 else []
            P.op("pool", (lambda n=n, c0=c0, c1=c1: lambda e: e.dma_start(out=wsc[n][:, c0:c1], in_=w32[n][:, c0:c1]))(),
                 reads=extra, writes=[("sc", n, k), ("cvslot", ci % 2)], dma=("cv", ci % 2))

        def load_cols(dst_buf, name, c0, ncols):
            rd = [("sc", name, k) for k, (a0, a1) in enumerate(wchunks[name]) if a0 < c0 + ncols and c0 < a1]
            assert rd
            P.op("sp", lambda e: e.dma_start(out=dst_buf.ap, in_=wsc[name][:, c0:c0 + ncols]),
                 reads=rd, writes=reslist(dst_buf.res), dma=reslist(dst_buf.res)[0])

        def load_panel(name, pidx):
            w = wring.next()
            load_cols(w, name, pidx * 1024, 1024)
            return w

        def mm_group(ps, pairs, reads, start=True, stop=True, out_ap=None):
            out_ap = ps.ap if out_ap is None else out_ap

            def fn(e):
                n = len(pairs)
                ins = None
                for i, (l, r) in enumerate(pairs):
                    ins = e.matmul(out_ap, lhsT=l, rhs=r, start=(start and i == 0), stop=(stop and i == n - 1))
                return ins
            P.op("pe", fn, reads=reads, writes=[ps.res])

        def rmsnorm_tile(g, ncol, final=False):
            ps = psC.next()
            for c in range(8):
                sq = t16.next()
                P.op("act", (lambda c=c, sq=sq: lambda e: e.activation(out=sq.ap, in_=X[:, c, tsl(g)], func=AF.Square))(),
                     reads=[xr(c, g)], writes=[sq.res])
                P.op("pe", (lambda c=c, sq=sq: lambda e: e.matmul(ps.ap, lhsT=onesm, rhs=sq.ap, start=(c == 0), stop=(c == 7)))(),
                     reads=[sq.res, "CMAT"], writes=[ps.res])
            R = t32.next()
            P.op("act", lambda e: e.activation(out=R.ap, in_=ps.ap, func=AF.Ln, bias=EPS), reads=[ps.res], writes=[R.res])
            P.op("act", lambda e: e.activation(out=R.ap, in_=R.ap, func=AF.Exp, scale=-0.5), reads=[R.res], writes=[R.res])
            for c in range(8):
                if final:
                    P.op("dve", (lambda c=c: lambda e: e.scalar_tensor_tensor(
                        out=X[:, c, tsl(g)], in0=X[:, c, tsl(g)], scalar=NRM[:, ncol * 8 + c:ncol * 8 + c + 1], in1=R.ap,
                        op0=ALU.mult, op1=ALU.mult))(), reads=[xr(c, g), R.res, "NRM"], writes=[xr(c, g)])
                else:
                    P.op("dve", (lambda c=c: lambda e: e.scalar_tensor_tensor(
                        out=H[:, c, tsl(g)], in0=X[:, c, tsl(g)], scalar=NRM[:, ncol * 8 + c:ncol * 8 + c + 1], in1=R.ap,
                        op0=ALU.mult, op1=ALU.mult))(), reads=[xr(c, g), R.res, "NRM"], writes=[hr(c, g)])

        def ffn_group(tiles, ng, nu, nd, ncol, hook=None):
            for g in tiles:
                rmsnorm_tile(g, ncol)
            for f in range(NF):
                if hook is not None and f == 6:
                    hook()
                wg = load_panel(ng, f)
                wu = load_panel(nu, f)
                for ti, g in enumerate(tiles):
                    pg, pu = psA.next(), psA.next()
                    hreads = [hr(kc, g) for kc in range(8)]
                    mm_group(pg, [(wg.ap[:, kc * 128:(kc + 1) * 128], H[:, kc, tsl(g)]) for kc in range(8)], [wg.res] + hreads)
                    mm_group(pu, [(wu.ap[:, kc * 128:(kc + 1) * 128], H[:, kc, tsl(g)]) for kc in range(8)], [wu.res] + hreads)
                    sg = t32.next()
                    P.op("act", (lambda pg=pg, sg=sg: lambda e: e.activation(out=sg.ap, in_=pg.ap, func=AF.Silu))(),
                         reads=[pg.res], writes=[sg.res])
                    col = f * 1024 + ti * 512
                    P.op("dve", (lambda pu=pu, sg=sg, col=col: lambda e: e.tensor_tensor(
                        out=BIG[:, col:col + 512], in0=sg.ap, in1=pu.ap, op=ALU.mult))(),
                        reads=[sg.res, pu.res], writes=bres(col // 512))
            for dc in range(8):
                wd = wdring.next()
                load_cols(wd, nd, dc * 2816, 2816)
                for ti, g in enumerate(tiles):
                    pd = psB.next()
                    mm_group(pd, [(wd.ap[:, f * 128:(f + 1) * 128], BIG[:, f * 1024 + ti * 512:f * 1024 + ti * 512 + 512])
                                  for f in range(NF)],
                             list(wd.res) + [r for f in range(NF) for r in bres(2 * f + ti)])
                    P.op("dve", (lambda pd=pd, dc=dc, g=g: lambda e: e.scalar_tensor_tensor(
                        out=X[:, dc, tsl(g)], in0=pd.ap, scalar=0.5, in1=X[:, dc, tsl(g)], op0=ALU.mult, op1=ALU.add))(),
                        reads=[pd.res, xr(dc, g)], writes=[xr(dc, g)])

        def ffn(ng, nu, nd, ncol, hook=None):
            for g0 in range(0, NT, 2):
                ffn_group(list(range(g0, min(g0 + 2, NT))), ng, nu, nd, ncol, hook if g0 == 0 else None)

        QT = MX1[:, 0:2048]
        KT = MX1[:, 2048:4096]

        def attention():
            zring = Ring([0, 1, 2])
            sparing = Ring(list(range(8)))
            ering = Ring([0, 1])
            bankring = Ring([Buf(PSB[i], ("ps", i)) for i in range(6)])

            def m2res(i, e_=None):
                if e_ is None:
                    return [("M2", 4 * i + k) for k in range(4)]
                return [("M2", 4 * i + 2 * e_), ("M2", 4 * i + 2 * e_ + 1)]

            def spa_ap(i):
                return MX2[:, i * 1024:(i + 1) * 1024].rearrange("q (e t) -> q e t", e=2)

            def proj_qk(hp, banks):
                wq = load_panel("win", 16 + hp)
                for g in range(NT):
                    ps = banks.next()
                    mm_group(ps, [(wq.ap[:, kc * 128:(kc + 1) * 128], H[:, kc, tsl(g)]) for kc in range(8)],
                             [wq.res] + [hr(kc, g) for kc in range(8)])
                    P.op("dve", (lambda ps=ps, g=g: lambda e: e.tensor_scalar(out=QT[:, tsl(g)], in0=ps.ap, scalar1=0.125, scalar2=None,
                                                                              op0=ALU.mult))(),
                         reads=[ps.res], writes=[("M1", g)])
                wk = load_panel("win", 24 + hp)
                for g in range(NT):
                    ps = banks.next()
                    mm_group(ps, [(wk.ap[:, kc * 128:(kc + 1) * 128], H[:, kc, tsl(g)]) for kc in range(8)],
                             [wk.res] + [hr(kc, g) for kc in range(8)])
                    P.op("dve", (lambda ps=ps, g=g: lambda e: e.tensor_copy(out=KT[:, tsl(g)], in_=ps.ap))(),
                         reads=[ps.res], writes=[("M1", 4 + g)])

            for hp in range(8):
                hpl = hp % 4
                if hpl == 0:
                    wring.i += (-wring.i) % 4
                    wv = [load_panel("win", 32 + hp + i) for i in range(4)]
                    assert [w.res for w in wv] == [("wr", i) for i in range(4)]
                    for blk in range(NBLK):
                        ps = bankring.next()
                        g = blk // 4
                        mm_group(ps, [(H[:, kc, blk * 128:(blk + 1) * 128], WR[:, 0:4, kc * 128:(kc + 1) * 128]) for kc in range(8)],
                                 [w.res for w in wv] + [hr(kc, g) for kc in range(8)])
                        col = 16384 + blk * 512
                        P.op("dve", (lambda ps=ps, col=col: lambda e: e.tensor_copy(out=BIG[:, col:col + 512], in_=ps.ap))(),
                             reads=[ps.res], writes=bres(col // 512))
                if hp == 0:
                    proj_qk(0, bankring)
                units = []
                for qt in range(NT):
                    nkb = 4 * (qt + 1)
                    for idx, j in enumerate(range(nkb - 1, -1, -1)):
                        c0 = 128 * (j - 4 * qt) if j >= 4 * qt else 0
                        units.append(dict(qt=qt, idx=idx, j=j, c0=c0, last=(idx == nkb - 1), gi=len(units)))
                yres = [("ps", 6), ("ps", 7)]

                def s1a(u):
                    zp = zring.next()
                    u["zp"] = zp
                    c0, j, qt = u["c0"], u["j"], u["qt"]
                    for e_ in range(2):
                        rows = slice(64 * e_, 64 * e_ + 64)
                        pairs = [(KT[rows, j * 128:(j + 1) * 128], QT[rows, qt * 512 + c0:(qt + 1) * 512])]
                        rds = [("M1", 4 + j // 4), ("M1", qt)]
                        if j >= 4 * qt:
                            pairs.append((ident, WIDE[:, 384:896 - c0]))
                            rds += ["CMAT", "WIDE"]
                        zb = Buf(PSB[2 * zp + e_][:, c0:512], ("ps", 2 * zp + e_))
                        mm_group(zb, pairs, rds, start=True, stop=False)
                    ei = ering.next()
                    u["ei"] = ei
                    P.op("act", lambda e: e.activation(out=T32[:, 2 * ei:2 * ei + 2, c0:512], in_=PP3[zp][:, :, c0:512], func=AF.Exp),
                         reads=[("ps", 2 * zp), ("ps", 2 * zp + 1)], writes=[("t32", 2 * ei), ("t32", 2 * ei + 1)])

                def s1b(u):
                    si = sparing.next()
                    u["sp"] = si
                    c0, ei = u["c0"], u["ei"]
                    P.op("act", lambda e: e.activation(out=spa_ap(si)[:, :, c0:512], in_=T32[:, 2 * ei:2 * ei + 2, c0:512], func=AF.Ln, bias=1.0),
                         reads=[("t32", 2 * ei), ("t32", 2 * ei + 1)], writes=m2res(si))

                def s2pe(u):
                    c0, zp, si, idx = u["c0"], u["zp"], u["sp"], u["idx"]
                    sv = u["gi"] % 2
                    nv = 1 - sv
                    for e_ in range(2):
                        pairs = [(uneg, spa_ap(si)[:, e_, c0:512])]
                        rds = m2res(si, e_) + ["CMAT"]
                        if idx > 0:
                            pairs.append((negones, T16[:, 2 * sv + e_, c0:512]))
                            rds.append(("t16", 2 * sv + e_))
                        zb = Buf(PSB[2 * zp + e_][:, c0:512], ("ps", 2 * zp + e_))
                        mm_group(zb, pairs, rds, start=False, stop=True)
                    if not u["last"]:
                        nres = [("t16", 2 * nv), ("t16", 2 * nv + 1)]
                        if c0 > 0:
                            P.op("dve", lambda e: e.memset(T16[:, 2 * nv:2 * nv + 2, 0:c0], 0.0), writes=nres)
                        if idx == 0:
                            P.op("dve", lambda e: e.tensor_copy(out=T16[:, 2 * nv:2 * nv + 2, c0:512], in_=spa_ap(si)[:, :, c0:512]),
                                 reads=m2res(si), writes=nres)
                        else:
                            P.op("dve", lambda e: e.tensor_tensor(out=T16[:, 2 * nv:2 * nv + 2, c0:512], in0=T16[:, 2 * sv:2 * sv + 2, c0:512],
                                                                  in1=spa_ap(si)[:, :, c0:512], op=ALU.add),
                                 reads=[("t16", 2 * sv), ("t16", 2 * sv + 1)] + m2res(si), writes=nres)

                def s2act(u):
                    ai = sparing.next()
                    u["a"] = ai
                    c0, zp = u["c0"], u["zp"]
                    P.op("act", lambda e: e.activation(out=spa_ap(ai)[:, :, c0:512], in_=PP3[zp][:, :, c0:512], func=AF.Exp),
                         reads=[("ps", 2 * zp), ("ps", 2 * zp + 1)], writes=m2res(ai))

                def s3(u):
                    c0, ai, j, idx, qt = u["c0"], u["a"], u["j"], u["idx"], u["qt"]
                    if idx == 0:
                        for e_ in range(2):
                            P.op("pe", (lambda e_=e_, qt=qt: lambda e: e.matmul(PSB[6 + e_], lhsT=zerom, rhs=QT[:, tsl(qt)], start=True, stop=False))(),
                                 reads=[("M1", qt), "CMAT"], writes=[yres[e_]])
                    vcol = 16384 + j * 512 + hpl * 128
                    for e_ in range(2):
                        yb = Buf(PSB[6 + e_][:, c0:512], yres[e_])
                        mm_group(yb, [(BIG[:, vcol:vcol + 128], spa_ap(ai)[:, e_, c0:512])], m2res(ai, e_) + bres(32 + j),
                                 start=False, stop=u["last"])
                    if u["last"]:
                        for e_ in range(2):
                            rows = slice(64 * e_, 64 * e_ + 64)
                            col = hp * S + qt * 512
                            P.op("dve", (lambda e_=e_, rows=rows, col=col: lambda e: e.tensor_copy(
                                out=BIG[rows, col:col + 512], in_=PSB[6 + e_][rows, :]))(),
                                reads=[yres[e_]], writes=[("B", col // 512, e_)])

                n = len(units)
                for i in range(n + 3):
                    if i == n and hp < 7:
                        fz = units[n - 3]["zp"]
                        proj_qk(hp + 1, Ring([Buf(PSB[2 * fz + k], ("ps", 2 * fz + k)) for k in range(2)]))
                    if i < n:
                        s1a(units[i])
                    if 1 <= i <= n:
                        s2pe(units[i - 1])
                    if 2 <= i <= n + 1:
                        s2act(units[i - 2])
                    if i < n:
                        s1b(units[i])
                    if i >= 3:
                        s3(units[i - 3])

        def yb_res(c, g):
            col = c * S + g * 512
            return [("B", col // 512, 0), ("B", col // 512, 1)]

        def mix_tile(g):
            hreads = [hr(kc, g) for kc in range(8)]
            vw = Buf(BIG[:, 16384:24576], tuple(r for k in range(32, 48) for r in bres(k)))
            load_cols(vw, "win", 8 * 1024, 8192)
            VW = BIG[:, 16384:24576].rearrange("p (n k) -> p n k", k=1024)
            def u_group(c):
                w = load_panel("win", c)
                ps = psA.next()
                mm_group(ps, [(w.ap[:, kc * 128:(kc + 1) * 128], H[:, kc, tsl(g)]) for kc in range(8)], [w.res] + hreads)
                P.op("act", (lambda ps=ps, c=c: lambda e: e.activation(out=MX1[:, c * 512:(c + 1) * 512], in_=ps.ap, func=AF.Gelu))(),
                     reads=[ps.res], writes=[("M1", c)])

            for tb in range(4):
                blk = 4 * g + tb
                par = tb % 2
                vfs = []
                for hf in range(2):
                    ps = psC.next()
                    mm_group(ps, [(H[:, kc, blk * 128:(blk + 1) * 128], VW[:, 4 * hf:4 * hf + 4, kc * 128:(kc + 1) * 128])
                                  for kc in range(8)], list(vw.res) + hreads)
                    VF = t32.next()
                    vfs.append(VF)
                    P.op("act", (lambda ps=ps, VF=VF: lambda e: e.activation(out=VF.ap, in_=ps.ap, func=AF.Gelu))(),
                         reads=[ps.res], writes=[VF.res])
                    for gg in range(2):
                        grp = 2 * hf + gg
                        P.op("dve", (lambda VF=VF, gg=gg, grp=grp, par=par: lambda e: e.bn_stats(
                            out=STT[:, par, grp, :], in_=VF.ap[:, gg * 256:(gg + 1) * 256]))(),
                            reads=[VF.res], writes=[("STT", par, grp)])
                        P.op("dve", (lambda grp=grp, par=par: lambda e: e.bn_aggr(out=MV[:, par, grp, :], in_=STT[:, par, grp, :]))(),
                             reads=[("STT", par, grp)], writes=[("MV", par, grp)])
                mvr = [("MV", par, grp) for grp in range(4)]
                P.op("act", (lambda par=par: lambda e: e.activation(out=RS[:, par, :], in_=MV[:, par, :, 1], func=AF.Ln, bias=EPS))(),
                     reads=mvr, writes=[("RS", par)])
                P.op("act", (lambda par=par: lambda e: e.activation(out=RS[:, par, :], in_=RS[:, par, :], func=AF.Exp, scale=-0.5))(),
                     reads=[("RS", par)], writes=[("RS", par)])
                for hf in range(2):
                    VF = vfs[hf]
                    for gg in range(2):
                        grp = 2 * hf + gg
                        P.op("dve", (lambda VF=VF, gg=gg, grp=grp, par=par: lambda e: e.tensor_scalar(
                            out=VF.ap[:, gg * 256:(gg + 1) * 256], in0=VF.ap[:, gg * 256:(gg + 1) * 256],
                            scalar1=MV[:, par, grp, 0:1], scalar2=RS[:, par, grp:grp + 1], op0=ALU.subtract, op1=ALU.mult))(),
                            reads=[VF.res, ("MV", par, grp), ("RS", par)], writes=[VF.res])
                    P.op("dve", (lambda VF=VF, hf=hf: lambda e: e.tensor_tensor(
                        out=VF.ap, in0=VF.ap, in1=LNG[:, hf * 512:(hf + 1) * 512], op=ALU.mult))(),
                        reads=[VF.res, "LNG"], writes=[VF.res])
                    vcol = 4096 + tb * 1024 + hf * 512
                    P.op("pool", (lambda VF=VF, hf=hf, vcol=vcol: lambda e: e.tensor_tensor(
                        out=MX2[:, vcol:vcol + 512], in0=VF.ap, in1=LNB[:, hf * 512:(hf + 1) * 512], op=ALU.add))(),
                        reads=[VF.res, "LNB"], writes=[("M2", vcol // 256), ("M2", vcol // 256 + 1)])
                u_group(2 * tb)
                u_group(2 * tb + 1)
            for c in range(8):
                grp = c // 2
                ps = psC.next()
                for tb in range(4):
                    vcol = 4096 + tb * 1024 + c * 128
                    P.op("pe", (lambda ps=ps, tb=tb, vcol=vcol, grp=grp: lambda e: e.matmul(
                        ps.ap[:, tb * 128:(tb + 1) * 128], lhsT=MX2[:, vcol:vcol + 128], rhs=WMT[:, grp * 128:(grp + 1) * 128],
                        start=True, stop=True))(),
                        reads=[("M2", vcol // 256), "WMT"], writes=[ps.res])
                T1 = t32.next()
                P.op("dve", (lambda ps=ps, grp=grp, T1=T1: lambda e: e.tensor_tensor(
                    out=T1.ap.rearrange("q (b t) -> q b t", b=4), in0=ps.ap.rearrange("q (b t) -> q b t", b=4),
                    in1=BSB[:, grp * 128:(grp + 1) * 128].unsqueeze(1).broadcast_to([128, 4, 128]), op=ALU.add))(),
                    reads=[ps.res, "BSB"], writes=[T1.res])
                P.op("dve", (lambda c=c, T1=T1: lambda e: e.tensor_tensor(
                    out=MX1[:, c * 512:(c + 1) * 512], in0=T1.ap, in1=MX1[:, c * 512:(c + 1) * 512], op=ALU.mult))(),
                    reads=[T1.res, ("M1", c)], writes=[("M1", c)])
            yar = [("M1", k) for k in range(8)]
            for c in range(8):
                wA = load_panel("wa", c)
                pA = psA.next()
                mm_group(pA, [(wA.ap[:, k * 128:(k + 1) * 128], MX1[:, k * 512:(k + 1) * 512]) for k in range(8)], [wA.res] + yar)
                wga = load_panel("win", 40 + c)
                pga = psA.next()
                mm_group(pga, [(wga.ap[:, kc * 128:(kc + 1) * 128], H[:, kc, tsl(g)]) for kc in range(8)], [wga.res] + hreads)
                GA = t32.next()
                P.op("act", (lambda pga=pga, GA=GA, c=c: lambda e: e.activation(out=GA.ap, in_=pga.ap, func=AF.Sigmoid, bias=BGT[:, c:c + 1]))(),
                     reads=[pga.res, "BGT"], writes=[GA.res])
                P.op("dve", (lambda pA=pA, GA=GA: lambda e: e.tensor_tensor(out=GA.ap, in0=GA.ap, in1=pA.ap, op=ALU.mult))(),
                     reads=[GA.res, pA.res], writes=[GA.res])
                wB = load_panel("wb", c)
                pB = psA.next()
                ybr = []
                for k in range(8):
                    ybr += yb_res(k, g)
                mm_group(pB, [(wB.ap[:, k * 128:(k + 1) * 128], BIG[:, k * S + g * 512:k * S + g * 512 + 512]) for k in range(8)],
                         [wB.res] + ybr)
                wgb = load_panel("win", 48 + c)
                pgb = psA.next()
                mm_group(pgb, [(wgb.ap[:, kc * 128:(kc + 1) * 128], H[:, kc, tsl(g)]) for kc in range(8)], [wgb.res] + hreads)
                GB = t32.next()
                P.op("act", (lambda pgb=pgb, GB=GB, c=c: lambda e: e.activation(out=GB.ap, in_=pgb.ap, func=AF.Sigmoid, bias=BGT[:, 8 + c:9 + c]))(),
                     reads=[pgb.res, "BGT"], writes=[GB.res])
                P.op("dve", (lambda pB=pB, GB=GB: lambda e: e.tensor_tensor(out=GB.ap, in0=GB.ap, in1=pB.ap, op=ALU.mult))(),
                     reads=[GB.res, pB.res], writes=[GB.res])
                P.op("dve", (lambda GA=GA, GB=GB, c=c: lambda e: e.tensor_tensor(
                    out=MX2[:, c * 512:(c + 1) * 512], in0=GA.ap, in1=GB.ap, op=ALU.add))(),
                    reads=[GA.res, GB.res], writes=[("M2", 2 * c), ("M2", 2 * c + 1)])
            mtr = [("M2", k) for k in range(16)]
            for c in range(8):
                wo = load_panel("wo", c)
                ps = psC.next()
                mm_group(ps, [(wo.ap[:, k * 128:(k + 1) * 128], MX2[:, k * 512:(k + 1) * 512]) for k in range(8)], [wo.res] + mtr)
                P.op("dve", (lambda ps=ps, c=c: lambda e: e.tensor_tensor(out=X[:, c, tsl(g)], in0=ps.ap, in1=X[:, c, tsl(g)], op=ALU.add))(),
                     reads=[ps.res, xr(c, g)], writes=[xr(c, g)])

        for b in range(NB):
            late = []
            for g in range(NT):
                if b == 0 and g in first_tiles:
                    continue
                if b == 0:
                    late.append(g)
                else:
                    load_x(b, g)
            ffn("f1g", "f1u", "f1d", 0, hook=(lambda b=b, late=late: [load_x(b, g) for g in late]))
            for g in range(NT):
                rmsnorm_tile(g, 1)
            attention()
            for g in range(NT):
                mix_tile(g)
            ffn("f2g", "f2u", "f2d", 2)
            for g in range(NT):
                rmsnorm_tile(g, 3, final=True)
                P.op("sp", (lambda b=b, g=g: lambda e: e.dma_start(
                    out=oT[b].rearrange("p (c s) -> p c s", s=S)[:, :, tsl(g)], in_=X[:, :, tsl(g)]))(),
                    reads=[xr(c, g) for c in range(8)], writes=[("out", b, g)], dma=("xout", g))
        P.op("sp", lambda e: None, reads=[("out", b, g) for b in range(NB) for g in range(NT)])
        P.emit(nc, st)
    return nc


def panelize(W):
    K, N = W.shape
    KC, NP = K // 128, N // 128
    return np.ascontiguousarray(W.reshape(KC, 128, NP, 128).transpose(1, 2, 0, 3)).reshape(128, NP * KC * 128)


def vec8(v):
    return np.ascontiguousarray(v.reshape(-1, 128).T)


def const_inputs():
    f = np.float32
    s = np.arange(128)
    cmat = np.zeros((128, 640), f)
    cmat[:, 0:128] = 1.0 / 1024.0
    cmat[:, 128:256] = np.eye(128, dtype=f)
    cmat[:, 256:384] = -(s[:, None] >= s[None, :]).astype(f)
    cmat[:, 384:512] = -1.0
    u = np.arange(896) - 384
    cmask = np.where(s[:, None] >= u[None, :], NEG, 0.0).astype(f)
    gm = np.ones((128, 128), f)
    gm[64:, :64] = 0.0
    gmask = np.tile(gm, (1, 4))
    return {"cmat": cmat, "cmask": cmask, "gmask": gmask}


def shared_inputs(inp):
    f = np.float32
    d = {}
    d["f1g"] = panelize(inp["ff1_w_gate"][0])
    d["f1u"] = panelize(inp["ff1_w_up"][0])
    d["f1d"] = panelize(inp["ff1_w_down"][0])
    d["f2g"] = panelize(inp["ff2_w_gate"][0])
    d["f2u"] = panelize(inp["ff2_w_up"][0])
    d["f2d"] = panelize(inp["ff2_w_down"][0])
    d["win"] = panelize(inp["w_in"][0])
    d["wa"] = panelize(inp["w_branch_a"][0])
    d["wb"] = panelize(inp["w_branch_b"][0])
    d["wo"] = panelize(inp["w_out"][0])
    d["norms"] = np.concatenate([vec8(inp["ff1_norm"][0]), vec8(inp["mix_norm"][0]), vec8(inp["ff2_norm"][0]),
                                 vec8(inp["final_norm"])], axis=1).astype(f)
    d["bgate"] = vec8(inp["b_gate"][0]).astype(f)
    d["lng"] = np.ascontiguousarray(np.broadcast_to(inp["gmlp_ln_g"][0].reshape(1, 1024), (128, 1024))).astype(f)
    d["lnb"] = np.ascontiguousarray(np.broadcast_to(inp["gmlp_ln_b"][0].reshape(1, 1024), (128, 1024))).astype(f)
    d["wst"] = np.ascontiguousarray(inp["gmlp_w_s"][0].transpose(2, 0, 1)).reshape(128, 512).astype(f)
    d["bsb"] = np.ascontiguousarray(np.broadcast_to(inp["gmlp_b_s"][0].reshape(1, 512), (128, 512))).astype(f)
    d.update(const_inputs())
    return d


_NC_CACHE = {}


def run(inp, n_cores, NB, S):
    inp = {k: np.asarray(v, dtype=np.float32) for k, v in inp.items()}
    x = inp["x"]
    assert x.shape == (n_cores * NB, S, D)
    shared = shared_inputs(inp)
    key = (NB, S)
    if key not in _NC_CACHE:
        _NC_CACHE[key] = build_nc(NB, S)
    nc = _NC_CACHE[key]
    in_maps = []
    for c in range(n_cores):
        xs = x[c * NB:(c + 1) * NB].reshape(NB, S, 8, 128).transpose(0, 3, 2, 1)
        m = dict(shared)
        m["xT"] = np.ascontiguousarray(xs).reshape(NB, 128, 8 * S)
        in_maps.append(m)
    res = run_bass_kernel_spmd(nc, in_maps, core_ids=list(range(n_cores)))
    outs = []
    for c in range(n_cores):
        o = np.asarray(res.results[c]["oT"]).reshape(NB, 128, 8, S).transpose(0, 3, 2, 1).reshape(NB, S, D)
        outs.append(o)
    return np.ascontiguousarray(np.concatenate(outs, axis=0)).astype(np.float32)


def kernel(**inputs):
    return run(inputs, 8, 4, 2048)
```

```python
import numpy as np
from contextlib import ExitStack
import concourse.bass as bass
import concourse.mybir as mybir
from concourse.bass_utils import run_bass_kernel_spmd

F32 = mybir.dt.float32
BF16 = mybir.dt.bfloat16
AF = mybir.ActivationFunctionType
ALU = mybir.AluOpType
ENGS = ("pe", "act", "dve", "pool", "sp")

D = 1024
DFF = 2816
NF = DFF // 128
EPS = 1e-6
NEG = -29952.0


class Op:
    __slots__ = ("eng", "fn", "deps", "signal", "semval", "dma", "dma_val")

    def __init__(self, eng, fn):
        self.eng = eng
        self.fn = fn
        self.deps = []
        self.signal = False
        self.semval = 0
        self.dma = None
        self.dma_val = 0


class Prog:
    def __init__(self):
        self.ops = {e: [] for e in ENGS}
        self.last_w = {}
        self.readers = {}
        self.dma_cnt = {}

    def op(self, eng, fn, reads=(), writes=(), dma=None):
        o = Op(eng, fn)
        deps = {}
        for r in reads:
            w = self.last_w.get(r)
            if w is not None:
                deps[id(w)] = (w, True)
        for r in writes:
            w = self.last_w.get(r)
            if w is not None and id(w) not in deps:
                deps[id(w)] = (w, False)
            for rd in self.readers.get(r, ()):
                if id(rd) not in deps:
                    deps[id(rd)] = (rd, False)
        for d, raw in deps.values():
            if d is o:
                continue
            if d.dma is not None:
                o.deps.append(d)
            elif d.eng != eng:
                d.signal = True
                o.deps.append(d)
            elif raw and eng != "pe":
                d.signal = True
                o.deps.append(d)
        for r in reads:
            self.readers.setdefault(r, []).append(o)
        for r in writes:
            self.last_w[r] = o
            self.readers[r] = []
        if dma is not None:
            o.dma = dma
            self.dma_cnt[dma] = self.dma_cnt.get(dma, 0) + 16
            o.dma_val = self.dma_cnt[dma]
        self.ops[eng].append(o)
        return o

    def emit(self, nc, stack):
        sems = {}
        for e in ENGS:
            sems[e] = stack.enter_context(nc.semaphore("s_" + e))
        for i, d in enumerate(self.dma_cnt):
            sems[("dma", d)] = stack.enter_context(nc.semaphore("d%d" % i))
        for e in ENGS:
            c = 0
            for o in self.ops[e]:
                if o.signal:
                    c += 1
                    o.semval = c
        block = stack.enter_context(nc.Block())

        def run(eng_name, eng):
            waited = {}
            for o in self.ops[eng_name]:
                need = {}
                for d in o.deps:
                    if d.dma is not None:
                        k, v = ("dma", d.dma), d.dma_val
                    else:
                        k, v = d.eng, d.semval
                    if waited.get(k, 0) >= v:
                        continue
                    if need.get(k, 0) < v:
                        need[k] = v
                for k, v in need.items():
                    eng.wait_ge(sems[k], v)
                    waited[k] = v
                ins = o.fn(eng)
                if ins is None:
                    continue
                if o.signal:
                    ins.then_inc(sems[eng_name], 1)
                if o.dma is not None:
                    ins.then_inc(sems[("dma", o.dma)], 16)

        @block.tensor
        def _(e):
            run("pe", e)

        @block.scalar
        def _(e):
            run("act", e)

        @block.vector
        def _(e):
            run("dve", e)

        @block.gpsimd
        def _(e):
            run("pool", e)

        @block.sync
        def _(e):
            run("sp", e)


class Ring:
    def __init__(self, items):
        self.items = items
        self.i = 0

    def next(self):
        it = self.items[self.i % len(self.items)]
        self.i += 1
        return it


class Buf:
    __slots__ = ("ap", "res")

    def __init__(self, ap, res):
        self.ap = ap
        self.res = res


W_SPECS = [
    ("f1g", 22528, 11264), ("f1u", 22528, 11264), ("f1d", 22528, 11264),
    ("win", 57344, 8192), ("wa", 8192, 8192), ("wb", 8192, 8192), ("wo", 8192, 8192),
    ("f2g", 22528, 11264), ("f2u", 22528, 11264), ("f2d", 22528, 11264),
]


def build_nc(NB, S):
    NT = S // 512
    NBLK = S // 128
    nc = bass.Bass("TRN2", target_bir_lowering=False)
    P = Prog()

    def din(name, shape, dt=F32):
        return nc.dram_tensor(name, shape, dt, kind="ExternalInput").ap()

    xT = din("xT", [NB, 128, 8 * S])
    oT = nc.dram_tensor("oT", [NB, 128, 8 * S], F32, kind="ExternalOutput").ap()
    w32 = {n: din(n, [128, c]) for n, c, _ in W_SPECS}
    wsc = {n: nc.dram_tensor("s_" + n, [128, c], BF16, kind="Internal").ap() for n, c, _ in W_SPECS}
    wchunk = {n: ch for n, _, ch in W_SPECS}
    d_norms = din("norms", [128, 32])
    d_bgate = din("bgate", [128, 16])
    d_lng = din("lng", [128, 1024])
    d_lnb = din("lnb", [128, 1024])
    d_wst = din("wst", [128, 512])
    d_bsb = din("bsb", [128, 512])
    d_cmat = din("cmat", [128, 640])
    d_cmask = din("cmask", [128, 896])
    d_gmask = din("gmask", [128, 512])

    with ExitStack() as st:
        def sb(name, shape, dt):
            return st.enter_context(nc.sbuf_tensor(name, shape, dt))

        X = sb("X", [128, 8, S], F32)
        H = sb("H", [128, 8, S], BF16)
        BIG = sb("BIG", [128, 24576], BF16)
        MX1 = sb("MX1", [128, 4096], BF16)
        MX2 = sb("MX2", [128, 8192], BF16)
        T32 = sb("T32", [128, 5, 512], F32)
        T16 = sb("T16", [128, 6, 512], BF16)
        WR = sb("WR", [128, 4, 1024], BF16)
        CMAT = sb("CMAT", [128, 640], BF16)
        WIDE = sb("WIDE", [128, 896], BF16)
        WMT = sb("WMT", [128, 512], BF16)
        BSB = sb("BSB", [128, 512], F32)
        LNG = sb("LNG", [128, 1024], F32)
        LNB = sb("LNB", [128, 1024], F32)
        NRM = sb("NRM", [128, 32], F32)
        BGT = sb("BGT", [128, 16], F32)
        STT = sb("STT", [128, 2, 4, 6], F32)
        MV = sb("MV", [128, 2, 4, 2], F32)
        RS = sb("RS", [128, 2, 4], F32)
        PP = [st.enter_context(nc.psum_tensor("pp%d" % i, [128, 1024], F32)) for i in range(4)]
        PSB = [PP[i // 2][:, (i % 2) * 512:(i % 2 + 1) * 512] for i in range(8)]
        PP3 = [PP[i][:].rearrange("q (e t) -> q e t", e=2) for i in range(4)]

        psA = Ring([Buf(PSB[i], ("ps", i)) for i in range(4)])
        psB = Ring([Buf(PSB[i], ("ps", i)) for i in (4, 5)])
        psC = Ring([Buf(PSB[i], ("ps", i)) for i in (6, 7)])
        t32 = Ring([Buf(T32[:, i, :], ("t32", i)) for i in range(5)])
        t16 = Ring([Buf(T16[:, i, :], ("t16", i)) for i in range(6)])
        wring = Ring([Buf(WR[:, i, :], ("wr", i)) for i in range(4)])
        wdring = Ring([Buf(MX2[:, i * 2816:(i + 1) * 2816], tuple(("M2", k) for k in range(11 * i, 11 * i + 11)))
                       for i in range(2)])
        onesm = CMAT[:, 0:128]
        ident = CMAT[:, 128:256]
        uneg = CMAT[:, 256:384]
        negones = CMAT[:, 384:512]
        zerom = CMAT[:, 512:640]

        def xr(c, g):
            return ("X", c, g)

        def hr(c, g):
            return ("H", c, g)

        def tsl(g):
            return slice(g * 512, (g + 1) * 512)

        def bres(k):
            return [("B", k, 0), ("B", k, 1)]

        def reslist(r):
            return list(r) if isinstance(r, tuple) and r and isinstance(r[0], tuple) else [r]

        def load_cast(dst_ap, src_ap, ncols, dres):
            done = 0
            while done < ncols:
                n = min(512, ncols - done)
                tb = t32.next()
                P.op("sp", (lambda tb=tb, done=done, n=n: lambda e: e.dma_start(out=tb.ap[:, 0:n], in_=src_ap[:, done:done + n]))(),
                     writes=[tb.res], dma=tb.res)
                P.op("dve", (lambda tb=tb, done=done, n=n: lambda e: e.tensor_copy(out=dst_ap[:, done:done + n], in_=tb.ap[:, 0:n]))(),
                     reads=[tb.res], writes=[dres])
                done += n

        load_cast(CMAT[:], d_cmat, 640, "CMAT")
        load_cast(WIDE[:], d_cmask, 896, "WIDE")
        for dst, src, nm in ((BSB, d_bsb, "BSB"), (LNG, d_lng, "LNG"), (LNB, d_lnb, "LNB"), (NRM, d_norms, "NRM"),
                             (BGT, d_bgate, "BGT")):
            P.op("sp", (lambda dst=dst, src=src: lambda e: e.dma_start(out=dst[:], in_=src))(), writes=[nm], dma=nm)
        ta, tb_ = t32.next(), t32.next()
        P.op("sp", lambda e: e.dma_start(out=ta.ap, in_=d_wst), writes=[ta.res], dma=ta.res)
        P.op("sp", lambda e: e.dma_start(out=tb_.ap, in_=d_gmask), writes=[tb_.res], dma=tb_.res)
        P.op("dve", lambda e: e.tensor_tensor(out=WMT[:], in0=ta.ap, in1=tb_.ap, op=ALU.mult),
             reads=[ta.res, tb_.res], writes=["WMT"])

        wchunks = {n: [] for n, _, _ in W_SPECS}
        conv_order = []

        def add_chunks(n, bounds):
            for c0, c1 in bounds:
                wchunks[n].append((c0, c1))
                conv_order.append((n, len(wchunks[n]) - 1, c0, c1))

        gb = [(0, 2048), (2048, 6144), (6144, 14336), (14336, 22528)]
        for (a0, a1) in gb:
            add_chunks("f1g", [(a0, a1)])
            add_chunks("f1u", [(a0, a1)])
        add_chunks("f1d", [(k * 5632, (k + 1) * 5632) for k in range(4)])
        add_chunks("win", [(k * 8192, (k + 1) * 8192) for k in (4, 2, 3)])
        add_chunks("win", [(k * 8192, (k + 1) * 8192) for k in (0, 1, 5, 6)])
        for n in ("wa", "wb", "wo"):
            add_chunks(n, [(0, 8192)])
        for n in ("f2g", "f2u"):
            add_chunks(n, [(k * 5632, (k + 1) * 5632) for k in range(4)])
        add_chunks("f2d", [(k * 5632, (k + 1) * 5632) for k in range(4)])
        def load_x(b, g):
            P.op("sp", (lambda b=b, g=g: lambda e: e.dma_start(
                out=X[:, :, tsl(g)], in_=xT[b].rearrange("p (c s) -> p c s", s=S)[:, :, tsl(g)]))(),
                writes=[xr(c, g) for c in range(8)] + [("xld", g)], dma=("xin", g))

        first_tiles = list(range(min(2, NT)))
        for g in first_tiles:
            load_x(0, g)
        for ci, (n, k, c0, c1) in enumerate(conv_order):
            extra = [("xld", g) for g in first_tiles] if ci >= 2 else []
            P.op("pool", (lambda n=n, c0=c0, c1=c1: lambda e: e.dma_start(out=wsc[n][:, c0:c1], in_=w32[n][:, c0:c1]))(),
                 reads=extra, writes=[("sc", n, k), ("cvslot", ci % 2)], dma=("cv", ci % 2))

        def load_cols(dst_buf, name, c0, ncols):
            rd = [("sc", name, k) for k, (a0, a1) in enumerate(wchunks[name]) if a0 < c0 + ncols and c0 < a1]
            assert rd
            P.op("sp", lambda e: e.dma_start(out=dst_buf.ap, in_=wsc[name][:, c0:c0 + ncols]),
                 reads=rd, writes=reslist(dst_buf.res), dma=reslist(dst_buf.res)[0])

        def load_panel(name, pidx):
            w = wring.next()
            load_cols(w, name, pidx * 1024, 1024)
            return w

        def mm_group(ps, pairs, reads, start=True, stop=True, out_ap=None):
            out_ap = ps.ap if out_ap is None else out_ap

            def fn(e):
                n = len(pairs)
                ins = None
                for i, (l, r) in enumerate(pairs):
                    ins = e.matmul(out_ap, lhsT=l, rhs=r, start=(start and i == 0), stop=(stop and i == n - 1))
                return ins
            P.op("pe", fn, reads=reads, writes=[ps.res])

        def rmsnorm_tile(g, ncol, final=False):
            ps = psC.next()
            for c in range(8):
                sq = t16.next()
                P.op("act", (lambda c=c, sq=sq: lambda e: e.activation(out=sq.ap, in_=X[:, c, tsl(g)], func=AF.Square))(),
                     reads=[xr(c, g)], writes=[sq.res])
                P.op("pe", (lambda c=c, sq=sq: lambda e: e.matmul(ps.ap, lhsT=onesm, rhs=sq.ap, start=(c == 0), stop=(c == 7)))(),
                     reads=[sq.res, "CMAT"], writes=[ps.res])
            R = t32.next()
            P.op("act", lambda e: e.activation(out=R.ap, in_=ps.ap, func=AF.Ln, bias=EPS), reads=[ps.res], writes=[R.res])
            P.op("act", lambda e: e.activation(out=R.ap, in_=R.ap, func=AF.Exp, scale=-0.5), reads=[R.res], writes=[R.res])
            for c in range(8):
                if final:
                    P.op("dve", (lambda c=c: lambda e: e.scalar_tensor_tensor(
                        out=X[:, c, tsl(g)], in0=X[:, c, tsl(g)], scalar=NRM[:, ncol * 8 + c:ncol * 8 + c + 1], in1=R.ap,
                        op0=ALU.mult, op1=ALU.mult))(), reads=[xr(c, g), R.res, "NRM"], writes=[xr(c, g)])
                else:
                    P.op("dve", (lambda c=c: lambda e: e.scalar_tensor_tensor(
                        out=H[:, c, tsl(g)], in0=X[:, c, tsl(g)], scalar=NRM[:, ncol * 8 + c:ncol * 8 + c + 1], in1=R.ap,
                        op0=ALU.mult, op1=ALU.mult))(), reads=[xr(c, g), R.res, "NRM"], writes=[hr(c, g)])

        def ffn_group(tiles, ng, nu, nd, ncol, hook=None, do_norm=True, pre_down=None):
            if do_norm:
                for g in tiles:
                    rmsnorm_tile(g, ncol)
            for f in range(NF):
                if hook is not None and f == 6:
                    hook()
                wg = load_panel(ng, f)
                wu = load_panel(nu, f)
                for ti, g in enumerate(tiles):
                    pg, pu = psA.next(), psA.next()
                    hreads = [hr(kc, g) for kc in range(8)]
                    mm_group(pg, [(wg.ap[:, kc * 128:(kc + 1) * 128], H[:, kc, tsl(g)]) for kc in range(8)], [wg.res] + hreads)
                    mm_group(pu, [(wu.ap[:, kc * 128:(kc + 1) * 128], H[:, kc, tsl(g)]) for kc in range(8)], [wu.res] + hreads)
                    sg = t32.next()
                    P.op("act", (lambda pg=pg, sg=sg: lambda e: e.activation(out=sg.ap, in_=pg.ap, func=AF.Silu))(),
                         reads=[pg.res], writes=[sg.res])
                    col = f * 1024 + ti * 512
                    P.op("dve", (lambda pu=pu, sg=sg, col=col: lambda e: e.tensor_tensor(
                        out=BIG[:, col:col + 512], in0=sg.ap, in1=pu.ap, op=ALU.mult))(),
                        reads=[sg.res, pu.res], writes=bres(col // 512))
            if pre_down is not None:
                pre_down()
            for dc in range(8):
                wd = wdring.next()
                load_cols(wd, nd, dc * 2816, 2816)
                for ti, g in enumerate(tiles):
                    pd = psB.next()
                    mm_group(pd, [(wd.ap[:, f * 128:(f + 1) * 128], BIG[:, f * 1024 + ti * 512:f * 1024 + ti * 512 + 512])
                                  for f in range(NF)],
                             list(wd.res) + [r for f in range(NF) for r in bres(2 * f + ti)])
                    P.op("dve", (lambda pd=pd, dc=dc, g=g: lambda e: e.scalar_tensor_tensor(
                        out=X[:, dc, tsl(g)], in0=pd.ap, scalar=0.5, in1=X[:, dc, tsl(g)], op0=ALU.mult, op1=ALU.add))(),
                        reads=[pd.res, xr(dc, g)], writes=[xr(dc, g)])

        def ffn(ng, nu, nd, ncol, hook=None, tail_norm=None):
            groups = [list(range(g0, min(g0 + 2, NT))) for g0 in range(0, NT, 2)]
            for k, tiles in enumerate(groups):
                if k + 1 < len(groups):
                    nxt = groups[k + 1]
                    pre = (lambda nxt=nxt: [rmsnorm_tile(g, ncol) for g in nxt])
                else:
                    pre = tail_norm
                ffn_group(tiles, ng, nu, nd, ncol, hook if k == 0 else None, do_norm=(k == 0), pre_down=pre)

        QT = MX1[:, 0:2048]
        KT = MX1[:, 2048:4096]

        def attention():
            zring = Ring([0, 1, 2])
            sparing = Ring(list(range(8)))
            ering = Ring([0, 1])
            bankring = Ring([Buf(PSB[i], ("ps", i)) for i in range(6)])

            def m2res(i, e_=None):
                if e_ is None:
                    return [("M2", 4 * i + k) for k in range(4)]
                return [("M2", 4 * i + 2 * e_), ("M2", 4 * i + 2 * e_ + 1)]

            def spa_ap(i):
                return MX2[:, i * 1024:(i + 1) * 1024].rearrange("q (e t) -> q e t", e=2)

            def proj_qk(hp, banks):
                wq = load_panel("win", 16 + hp)
                for g in range(NT):
                    ps = banks.next()
                    mm_group(ps, [(wq.ap[:, kc * 128:(kc + 1) * 128], H[:, kc, tsl(g)]) for kc in range(8)],
                             [wq.res] + [hr(kc, g) for kc in range(8)])
                    P.op("dve", (lambda ps=ps, g=g: lambda e: e.tensor_scalar(out=QT[:, tsl(g)], in0=ps.ap, scalar1=0.125, scalar2=None,
                                                                              op0=ALU.mult))(),
                         reads=[ps.res], writes=[("M1", g)])
                wk = load_panel("win", 24 + hp)
                for g in range(NT):
                    ps = banks.next()
                    mm_group(ps, [(wk.ap[:, kc * 128:(kc + 1) * 128], H[:, kc, tsl(g)]) for kc in range(8)],
                             [wk.res] + [hr(kc, g) for kc in range(8)])
                    P.op("dve", (lambda ps=ps, g=g: lambda e: e.tensor_copy(out=KT[:, tsl(g)], in_=ps.ap))(),
                         reads=[ps.res], writes=[("M1", 4 + g)])

            for hp in range(8):
                hpl = hp % 4
                if hpl == 0:
                    wring.i += (-wring.i) % 4
                    wv = [load_panel("win", 32 + hp + i) for i in range(4)]
                    assert [w.res for w in wv] == [("wr", i) for i in range(4)]
                    for blk in range(NBLK):
                        ps = bankring.next()
                        g = blk // 4
                        mm_group(ps, [(H[:, kc, blk * 128:(blk + 1) * 128], WR[:, 0:4, kc * 128:(kc + 1) * 128]) for kc in range(8)],
                                 [w.res for w in wv] + [hr(kc, g) for kc in range(8)])
                        col = 16384 + blk * 512
                        P.op("dve", (lambda ps=ps, col=col: lambda e: e.tensor_copy(out=BIG[:, col:col + 512], in_=ps.ap))(),
                             reads=[ps.res], writes=bres(col // 512))
                if hp == 0:
                    proj_qk(0, bankring)
                units = []
                for qt in range(NT):
                    nkb = 4 * (qt + 1)
                    for idx, j in enumerate(range(nkb - 1, -1, -1)):
                        c0 = 128 * (j - 4 * qt) if j >= 4 * qt else 0
                        units.append(dict(qt=qt, idx=idx, j=j, c0=c0, last=(idx == nkb - 1), gi=len(units)))
                yres = [("ps", 6), ("ps", 7)]

                def s1a(u):
                    zp = zring.next()
                    u["zp"] = zp
                    c0, j, qt = u["c0"], u["j"], u["qt"]
                    for e_ in range(2):
                        rows = slice(64 * e_, 64 * e_ + 64)
                        pairs = [(KT[rows, j * 128:(j + 1) * 128], QT[rows, qt * 512 + c0:(qt + 1) * 512])]
                        rds = [("M1", 4 + j // 4), ("M1", qt)]
                        if j >= 4 * qt:
                            pairs.append((ident, WIDE[:, 384:896 - c0]))
                            rds += ["CMAT", "WIDE"]
                        zb = Buf(PSB[2 * zp + e_][:, c0:512], ("ps", 2 * zp + e_))
                        mm_group(zb, pairs, rds, start=True, stop=False)
                    ei = ering.next()
                    u["ei"] = ei
                    P.op("act", lambda e: e.activation(out=T32[:, 2 * ei:2 * ei + 2, c0:512], in_=PP3[zp][:, :, c0:512], func=AF.Exp),
                         reads=[("ps", 2 * zp), ("ps", 2 * zp + 1)], writes=[("t32", 2 * ei), ("t32", 2 * ei + 1)])

                def s1b(u):
                    si = sparing.next()
                    u["sp"] = si
                    c0, ei = u["c0"], u["ei"]
                    P.op("act", lambda e: e.activation(out=spa_ap(si)[:, :, c0:512], in_=T32[:, 2 * ei:2 * ei + 2, c0:512], func=AF.Ln, bias=1.0),
                         reads=[("t32", 2 * ei), ("t32", 2 * ei + 1)], writes=m2res(si))

                def s2pe(u):
                    c0, zp, si, idx = u["c0"], u["zp"], u["sp"], u["idx"]
                    sv = u["gi"] % 2
                    nv = 1 - sv
                    for e_ in range(2):
                        pairs = [(uneg, spa_ap(si)[:, e_, c0:512])]
                        rds = m2res(si, e_) + ["CMAT"]
                        if idx > 0:
                            pairs.append((negones, T16[:, 2 * sv + e_, c0:512]))
                            rds.append(("t16", 2 * sv + e_))
                        zb = Buf(PSB[2 * zp + e_][:, c0:512], ("ps", 2 * zp + e_))
                        mm_group(zb, pairs, rds, start=False, stop=True)
                    if not u["last"]:
                        nres = [("t16", 2 * nv), ("t16", 2 * nv + 1)]
                        if c0 > 0:
                            P.op("dve", lambda e: e.memset(T16[:, 2 * nv:2 * nv + 2, 0:c0], 0.0), writes=nres)
                        if idx == 0:
                            P.op("dve", lambda e: e.tensor_copy(out=T16[:, 2 * nv:2 * nv + 2, c0:512], in_=spa_ap(si)[:, :, c0:512]),
                                 reads=m2res(si), writes=nres)
                        else:
                            P.op("dve", lambda e: e.tensor_tensor(out=T16[:, 2 * nv:2 * nv + 2, c0:512], in0=T16[:, 2 * sv:2 * sv + 2, c0:512],
                                                                  in1=spa_ap(si)[:, :, c0:512], op=ALU.add),
                                 reads=[("t16", 2 * sv), ("t16", 2 * sv + 1)] + m2res(si), writes=nres)

                def s2act(u):
                    ai = sparing.next()
                    u["a"] = ai
                    c0, zp = u["c0"], u["zp"]
                    P.op("act", lambda e: e.activation(out=spa_ap(ai)[:, :, c0:512], in_=PP3[zp][:, :, c0:512], func=AF.Exp),
                         reads=[("ps", 2 * zp), ("ps", 2 * zp + 1)], writes=m2res(ai))

                def s3(u):
                    c0, ai, j, idx, qt = u["c0"], u["a"], u["j"], u["idx"], u["qt"]
                    if idx == 0:
                        for e_ in range(2):
                            P.op("pe", (lambda e_=e_, qt=qt: lambda e: e.matmul(PSB[6 + e_], lhsT=zerom, rhs=QT[:, tsl(qt)], start=True, stop=False))(),
                                 reads=[("M1", qt), "CMAT"], writes=[yres[e_]])
                    vcol = 16384 + j * 512 + hpl * 128
                    for e_ in range(2):
                        yb = Buf(PSB[6 + e_][:, c0:512], yres[e_])
                        mm_group(yb, [(BIG[:, vcol:vcol + 128], spa_ap(ai)[:, e_, c0:512])], m2res(ai, e_) + bres(32 + j),
                                 start=False, stop=u["last"])
                    if u["last"]:
                        for e_ in range(2):
                            rows = slice(64 * e_, 64 * e_ + 64)
                            col = hp * S + qt * 512
                            P.op("dve", (lambda e_=e_, rows=rows, col=col: lambda e: e.tensor_copy(
                                out=BIG[rows, col:col + 512], in_=PSB[6 + e_][rows, :]))(),
                                reads=[yres[e_]], writes=[("B", col // 512, e_)])

                n = len(units)
                for i in range(n + 3):
                    if i == n and hp < 7:
                        fz = units[n - 3]["zp"]
                        proj_qk(hp + 1, Ring([Buf(PSB[2 * fz + k], ("ps", 2 * fz + k)) for k in range(2)]))
                    if i < n:
                        s1a(units[i])
                    if 1 <= i <= n:
                        s2pe(units[i - 1])
                    if 2 <= i <= n + 1:
                        s2act(units[i - 2])
                    if i < n:
                        s1b(units[i])
                    if i >= 3:
                        s3(units[i - 3])

        def yb_res(c, g):
            col = c * S + g * 512
            return [("B", col // 512, 0), ("B", col // 512, 1)]

        def mix_tile(g):
            hreads = [hr(kc, g) for kc in range(8)]
            vw = Buf(BIG[:, 16384:24576], tuple(r for k in range(32, 48) for r in bres(k)))
            load_cols(vw, "win", 8 * 1024, 8192)
            VW = BIG[:, 16384:24576].rearrange("p (n k) -> p n k", k=1024)
            def u_group(c):
                w = load_panel("win", c)
                ps = psA.next()
                mm_group(ps, [(w.ap[:, kc * 128:(kc + 1) * 128], H[:, kc, tsl(g)]) for kc in range(8)], [w.res] + hreads)
                P.op("act", (lambda ps=ps, c=c: lambda e: e.activation(out=MX1[:, c * 512:(c + 1) * 512], in_=ps.ap, func=AF.Gelu))(),
                     reads=[ps.res], writes=[("M1", c)])

            for tb in range(4):
                blk = 4 * g + tb
                par = tb % 2
                vfs = []
                for hf in range(2):
                    ps = psC.next()
                    mm_group(ps, [(H[:, kc, blk * 128:(blk + 1) * 128], VW[:, 4 * hf:4 * hf + 4, kc * 128:(kc + 1) * 128])
                                  for kc in range(8)], list(vw.res) + hreads)
                    VF = t32.next()
                    vfs.append(VF)
                    P.op("act", (lambda ps=ps, VF=VF: lambda e: e.activation(out=VF.ap, in_=ps.ap, func=AF.Gelu))(),
                         reads=[ps.res], writes=[VF.res])
                    for gg in range(2):
                        grp = 2 * hf + gg
                        P.op("dve", (lambda VF=VF, gg=gg, grp=grp, par=par: lambda e: e.bn_stats(
                            out=STT[:, par, grp, :], in_=VF.ap[:, gg * 256:(gg + 1) * 256]))(),
                            reads=[VF.res], writes=[("STT", par, grp)])
                        P.op("dve", (lambda grp=grp, par=par: lambda e: e.bn_aggr(out=MV[:, par, grp, :], in_=STT[:, par, grp, :]))(),
                             reads=[("STT", par, grp)], writes=[("MV", par, grp)])
                mvr = [("MV", par, grp) for grp in range(4)]
                P.op("act", (lambda par=par: lambda e: e.activation(out=RS[:, par, :], in_=MV[:, par, :, 1], func=AF.Ln, bias=EPS))(),
                     reads=mvr, writes=[("RS", par)])
                P.op("act", (lambda par=par: lambda e: e.activation(out=RS[:, par, :], in_=RS[:, par, :], func=AF.Exp, scale=-0.5))(),
                     reads=[("RS", par)], writes=[("RS", par)])
                for hf in range(2):
                    VF = vfs[hf]
                    for gg in range(2):
                        grp = 2 * hf + gg
                        P.op("dve", (lambda VF=VF, gg=gg, grp=grp, par=par: lambda e: e.tensor_scalar(
                            out=VF.ap[:, gg * 256:(gg + 1) * 256], in0=VF.ap[:, gg * 256:(gg + 1) * 256],
                            scalar1=MV[:, par, grp, 0:1], scalar2=RS[:, par, grp:grp + 1], op0=ALU.subtract, op1=ALU.mult))(),
                            reads=[VF.res, ("MV", par, grp), ("RS", par)], writes=[VF.res])
                    P.op("dve", (lambda VF=VF, hf=hf: lambda e: e.tensor_tensor(
                        out=VF.ap, in0=VF.ap, in1=LNG[:, hf * 512:(hf + 1) * 512], op=ALU.mult))(),
                        reads=[VF.res, "LNG"], writes=[VF.res])
                    vcol = 4096 + tb * 1024 + hf * 512
                    P.op("pool", (lambda VF=VF, hf=hf, vcol=vcol: lambda e: e.tensor_tensor(
                        out=MX2[:, vcol:vcol + 512], in0=VF.ap, in1=LNB[:, hf * 512:(hf + 1) * 512], op=ALU.add))(),
                        reads=[VF.res, "LNB"], writes=[("M2", vcol // 256), ("M2", vcol // 256 + 1)])
                u_group(2 * tb)
                u_group(2 * tb + 1)
            for c in range(8):
                grp = c // 2
                ps = psC.next()
                for tb in range(4):
                    vcol = 4096 + tb * 1024 + c * 128
                    P.op("pe", (lambda ps=ps, tb=tb, vcol=vcol, grp=grp: lambda e: e.matmul(
                        ps.ap[:, tb * 128:(tb + 1) * 128], lhsT=MX2[:, vcol:vcol + 128], rhs=WMT[:, grp * 128:(grp + 1) * 128],
                        start=True, stop=True))(),
                        reads=[("M2", vcol // 256), "WMT"], writes=[ps.res])
                T1 = t32.next()
                P.op("dve", (lambda ps=ps, grp=grp, T1=T1: lambda e: e.tensor_tensor(
                    out=T1.ap.rearrange("q (b t) -> q b t", b=4), in0=ps.ap.rearrange("q (b t) -> q b t", b=4),
                    in1=BSB[:, grp * 128:(grp + 1) * 128].unsqueeze(1).broadcast_to([128, 4, 128]), op=ALU.add))(),
                    reads=[ps.res, "BSB"], writes=[T1.res])
                P.op("dve", (lambda c=c, T1=T1: lambda e: e.tensor_tensor(
                    out=MX1[:, c * 512:(c + 1) * 512], in0=T1.ap, in1=MX1[:, c * 512:(c + 1) * 512], op=ALU.mult))(),
                    reads=[T1.res, ("M1", c)], writes=[("M1", c)])
            yar = [("M1", k) for k in range(8)]
            for c in range(8):
                wA = load_panel("wa", c)
                pA = psA.next()
                mm_group(pA, [(wA.ap[:, k * 128:(k + 1) * 128], MX1[:, k * 512:(k + 1) * 512]) for k in range(8)], [wA.res] + yar)
                wga = load_panel("win", 40 + c)
                pga = psA.next()
                mm_group(pga, [(wga.ap[:, kc * 128:(kc + 1) * 128], H[:, kc, tsl(g)]) for kc in range(8)], [wga.res] + hreads)
                GA = t32.next()
                P.op("act", (lambda pga=pga, GA=GA, c=c: lambda e: e.activation(out=GA.ap, in_=pga.ap, func=AF.Sigmoid, bias=BGT[:, c:c + 1]))(),
                     reads=[pga.res, "BGT"], writes=[GA.res])
                P.op("dve", (lambda pA=pA, GA=GA: lambda e: e.tensor_tensor(out=GA.ap, in0=GA.ap, in1=pA.ap, op=ALU.mult))(),
                     reads=[GA.res, pA.res], writes=[GA.res])
                wB = load_panel("wb", c)
                pB = psA.next()
                ybr = []
                for k in range(8):
                    ybr += yb_res(k, g)
                mm_group(pB, [(wB.ap[:, k * 128:(k + 1) * 128], BIG[:, k * S + g * 512:k * S + g * 512 + 512]) for k in range(8)],
                         [wB.res] + ybr)
                wgb = load_panel("win", 48 + c)
                pgb = psA.next()
                mm_group(pgb, [(wgb.ap[:, kc * 128:(kc + 1) * 128], H[:, kc, tsl(g)]) for kc in range(8)], [wgb.res] + hreads)
                GB = t32.next()
                P.op("act", (lambda pgb=pgb, GB=GB, c=c: lambda e: e.activation(out=GB.ap, in_=pgb.ap, func=AF.Sigmoid, bias=BGT[:, 8 + c:9 + c]))(),
                     reads=[pgb.res, "BGT"], writes=[GB.res])
                P.op("dve", (lambda pB=pB, GB=GB: lambda e: e.tensor_tensor(out=GB.ap, in0=GB.ap, in1=pB.ap, op=ALU.mult))(),
                     reads=[GB.res, pB.res], writes=[GB.res])
                P.op("dve", (lambda GA=GA, GB=GB, c=c: lambda e: e.tensor_tensor(
                    out=MX2[:, c * 512:(c + 1) * 512], in0=GA.ap, in1=GB.ap, op=ALU.add))(),
                    reads=[GA.res, GB.res], writes=[("M2", 2 * c), ("M2", 2 * c + 1)])
            mtr = [("M2", k) for k in range(16)]
            for c in range(8):
                wo = load_panel("wo", c)
                ps = psC.next()
                mm_group(ps, [(wo.ap[:, k * 128:(k + 1) * 128], MX2[:, k * 512:(k + 1) * 512]) for k in range(8)], [wo.res] + mtr)
                P.op("dve", (lambda ps=ps, c=c: lambda e: e.tensor_tensor(out=X[:, c, tsl(g)], in0=ps.ap, in1=X[:, c, tsl(g)], op=ALU.add))(),
                     reads=[ps.res, xr(c, g)], writes=[xr(c, g)])

        for b in range(NB):
            late = []
            for g in range(NT):
                if b == 0 and g in first_tiles:
                    continue
                if b == 0:
                    late.append(g)
                else:
                    load_x(b, g)
            early = list(range(0, NT - 2)) if NT > 2 else []
            ffn("f1g", "f1u", "f1d", 0, hook=(lambda b=b, late=late: [load_x(b, g) for g in late]),
                tail_norm=(lambda early=early: [rmsnorm_tile(g, 1) for g in early]))
            for g in range(NT):
                if g not in early:
                    rmsnorm_tile(g, 1)
            attention()
            for g in range(NT):
                mix_tile(g)
            ffn("f2g", "f2u", "f2d", 2)
            for g in range(NT):
                rmsnorm_tile(g, 3, final=True)
                P.op("sp", (lambda b=b, g=g: lambda e: e.dma_start(
                    out=oT[b].rearrange("p (c s) -> p c s", s=S)[:, :, tsl(g)], in_=X[:, :, tsl(g)]))(),
                    reads=[xr(c, g) for c in range(8)], writes=[("out", b, g)], dma=("xout", g))
        P.op("sp", lambda e: None, reads=[("out", b, g) for b in range(NB) for g in range(NT)])
        P.emit(nc, st)
    return nc


def panelize(W):
    K, N = W.shape
    KC, NP = K // 128, N // 128
    return np.ascontiguousarray(W.reshape(KC, 128, NP, 128).transpose(1, 2, 0, 3)).reshape(128, NP * KC * 128)


def vec8(v):
    return np.ascontiguousarray(v.reshape(-1, 128).T)


def const_inputs():
    f = np.float32
    s = np.arange(128)
    cmat = np.zeros((128, 640), f)
    cmat[:, 0:128] = 1.0 / 1024.0
    cmat[:, 128:256] = np.eye(128, dtype=f)
    cmat[:, 256:384] = -(s[:, None] >= s[None, :]).astype(f)
    cmat[:, 384:512] = -1.0
    u = np.arange(896) - 384
    cmask = np.where(s[:, None] >= u[None, :], NEG, 0.0).astype(f)
    gm = np.ones((128, 128), f)
    gm[64:, :64] = 0.0
    gmask = np.tile(gm, (1, 4))
    return {"cmat": cmat, "cmask": cmask, "gmask": gmask}


def shared_inputs(inp):
    f = np.float32
    d = {}
    d["f1g"] = panelize(inp["ff1_w_gate"][0])
    d["f1u"] = panelize(inp["ff1_w_up"][0])
    d["f1d"] = panelize(inp["ff1_w_down"][0])
    d["f2g"] = panelize(inp["ff2_w_gate"][0])
    d["f2u"] = panelize(inp["ff2_w_up"][0])
    d["f2d"] = panelize(inp["ff2_w_down"][0])
    d["win"] = panelize(inp["w_in"][0])
    d["wa"] = panelize(inp["w_branch_a"][0])
    d["wb"] = panelize(inp["w_branch_b"][0])
    d["wo"] = panelize(inp["w_out"][0])
    d["norms"] = np.concatenate([vec8(inp["ff1_norm"][0]), vec8(inp["mix_norm"][0]), vec8(inp["ff2_norm"][0]),
                                 vec8(inp["final_norm"])], axis=1).astype(f)
    d["bgate"] = vec8(inp["b_gate"][0]).astype(f)
    d["lng"] = np.ascontiguousarray(np.broadcast_to(inp["gmlp_ln_g"][0].reshape(1, 1024), (128, 1024))).astype(f)
    d["lnb"] = np.ascontiguousarray(np.broadcast_to(inp["gmlp_ln_b"][0].reshape(1, 1024), (128, 1024))).astype(f)
    d["wst"] = np.ascontiguousarray(inp["gmlp_w_s"][0].transpose(2, 0, 1)).reshape(128, 512).astype(f)
    d["bsb"] = np.ascontiguousarray(np.broadcast_to(inp["gmlp_b_s"][0].reshape(1, 512), (128, 512))).astype(f)
    d.update(const_inputs())
    return d


_NC_CACHE = {}


def run(inp, n_cores, NB, S):
    inp = {k: np.asarray(v, dtype=np.float32) for k, v in inp.items()}
    x = inp["x"]
    assert x.shape == (n_cores * NB, S, D)
    shared = shared_inputs(inp)
    key = (NB, S)
    if key not in _NC_CACHE:
        _NC_CACHE[key] = build_nc(NB, S)
    nc = _NC_CACHE[key]
    in_maps = []
    for c in range(n_cores):
        xs = x[c * NB:(c + 1) * NB].reshape(NB, S, 8, 128).transpose(0, 3, 2, 1)
        m = dict(shared)
        m["xT"] = np.ascontiguousarray(xs).reshape(NB, 128, 8 * S)
        in_maps.append(m)
    res = run_bass_kernel_spmd(nc, in_maps, core_ids=list(range(n_cores)))
    outs = []
    for c in range(n_cores):
        o = np.asarray(res.results[c]["oT"]).reshape(NB, 128, 8, S).transpose(0, 3, 2, 1).reshape(NB, S, D)
        outs.append(o)
    return np.ascontiguousarray(np.concatenate(outs, axis=0)).astype(np.float32)


def kernel(**inputs):
    return run(inputs, 8, 4, 2048)
```
